# Optimizing a Trainium2 kernel written in Bass

```python
import math
import jax, jax.numpy as jnp
from jax import lax
import numpy as np

D_MODEL = 2048
BATCH = 8
SEQ = 4096
DEPTH = 2

HEAD_DIM = 128
UNIT = D_MODEL // (4 * HEAD_DIM)
GMLP_GROUPS = UNIT
GMLP_CHUNK = 128
GMLP_WIDTH = GMLP_GROUPS * HEAD_DIM
DIFF_HEADS = UNIT
DIFF_QK_WIDTH = DIFF_HEADS * 2 * HEAD_DIM
DIFF_V_DIM = 2 * HEAD_DIM
DIFF_WIDTH = DIFF_HEADS * DIFF_V_DIM
DIFF_Q_BLOCK = 128
CONV_GROUPS = UNIT
CONV_WIDTH = CONV_GROUPS * HEAD_DIM
CONV_K = 3
MIX_WIDTH = GMLP_WIDTH + DIFF_WIDTH + CONV_WIDTH
IN_PROJ_WIDTH = 2 * GMLP_WIDTH + 2 * DIFF_QK_WIDTH + DIFF_WIDTH + 3 * CONV_WIDTH
SPLIT_IDX = [2 * GMLP_WIDTH,
             2 * GMLP_WIDTH + DIFF_QK_WIDTH,
             2 * GMLP_WIDTH + 2 * DIFF_QK_WIDTH,
             2 * GMLP_WIDTH + 2 * DIFF_QK_WIDTH + DIFF_WIDTH]
D_FF = ((8 * D_MODEL + 3 * 256 - 1) // (3 * 256)) * 256
ROPE_THETA = 10000.0
RMS_EPS = 1e-6
LN_EPS = 1e-5

kernel_name = "hymba_gmlp_diffattn_shortconv_block"


def rms_norm(x, g):
    xf = x.astype(jnp.float32)
    y = xf * lax.rsqrt(jnp.mean(xf * xf, axis=-1, keepdims=True) + RMS_EPS)
    return (y * g.astype(jnp.float32)).astype(x.dtype)


def group_rms(x):
    xf = x.astype(jnp.float32)
    return (xf * lax.rsqrt(jnp.mean(xf * xf, axis=-1, keepdims=True) + RMS_EPS)).astype(x.dtype)


def rope(x, cos, sin):
    xf = x.astype(jnp.float32)
    x1, x2 = jnp.split(xf, 2, axis=-1)
    out = jnp.concatenate([x1 * cos - x2 * sin, x2 * cos + x1 * sin], axis=-1)
    return out.astype(x.dtype)


def gmlp_mixer(z, ln_g, ln_b, ws, bs):
    b_, s_, _ = z.shape
    z = jax.nn.gelu(z, approximate=False).reshape(b_, s_, 2, GMLP_GROUPS, HEAD_DIM)
    u, v = z[:, :, 0], z[:, :, 1]
    vf = v.astype(jnp.float32)
    mu = jnp.mean(vf, axis=-1, keepdims=True)
    var = jnp.mean(jnp.square(vf - mu), axis=-1, keepdims=True)
    vn = ((vf - mu) * lax.rsqrt(var + LN_EPS) * ln_g.astype(jnp.float32)
          + ln_b.astype(jnp.float32)).astype(v.dtype)
    vn = vn.reshape(b_, s_ // GMLP_CHUNK, GMLP_CHUNK, GMLP_GROUPS, HEAD_DIM)
    causal = jnp.tril(jnp.ones((GMLP_CHUNK, GMLP_CHUNK), dtype=bool))
    w = jnp.where(causal[None], ws, jnp.zeros_like(ws))
    mixed = jnp.einsum('gts,bnsgc->bntgc', w, vn) + jnp.transpose(bs)[:, :, None]
    return u * mixed.reshape(b_, s_, GMLP_GROUPS, HEAD_DIM)


def diff_attention(q, k, v, lam, cos, sin):
    b_, s_, h_, _, d_ = q.shape
    nb = s_ // DIFF_Q_BLOCK
    q = rope(q, cos, sin) * (1.0 / math.sqrt(d_))
    k = rope(k, cos, sin)
    qb = q.reshape(b_, nb, DIFF_Q_BLOCK, h_, 2, d_).transpose(1, 0, 3, 4, 2, 5)
    kt = k.transpose(0, 2, 3, 1, 4)
    vt = v.transpose(0, 2, 1, 3)
    key_pos = jnp.arange(s_)

    def block(args):
        q_blk, i = args
        s = jnp.einsum('bhmqd,bhmkd->bhmqk', q_blk, kt).astype(jnp.float32)
        q_pos = i * DIFF_Q_BLOCK + jnp.arange(DIFF_Q_BLOCK)
        causal = key_pos[None, :] <= q_pos[:, None]
        s = jnp.where(causal, s, -jnp.inf)
        p = jax.nn.softmax(s, axis=-1)
        a = p[:, :, 0] - lam * p[:, :, 1]
        return jnp.einsum('bhqk,bhkd->bhqd', a.astype(vt.dtype), vt)

    o = lax.map(block, (qb, jnp.arange(nb)))
    return o.transpose(1, 0, 3, 2, 4).reshape(b_, s_, h_, 2 * d_)


def short_conv_mixer(bg, cg, hc, conv_w):
    xh = cg * hc
    y = lax.conv_general_dilated(
        xh, conv_w[:, None, :].astype(xh.dtype), window_strides=(1,),
        padding=[(CONV_K - 1, 0)], dimension_numbers=('NWC', 'WIO', 'NWC'),
        feature_group_count=CONV_WIDTH)
    return bg * y


def setup_inputs(seed: int = 0) -> dict:
    key = jax.random.key(seed)
    ks = jax.random.split(key, 20)
    f32 = jnp.float32
    nrm = lambda k, shape, scale: jax.random.normal(k, shape, f32) * scale
    return {
        "x": nrm(ks[0], (BATCH, SEQ, D_MODEL), 1.0),
        "positions": jnp.broadcast_to(jnp.arange(SEQ, dtype=jnp.int32), (BATCH, SEQ)),
        "attn_norm": 1.0 + nrm(ks[1], (DEPTH, D_MODEL), 0.02),
        "w_in": nrm(ks[2], (DEPTH, D_MODEL, IN_PROJ_WIDTH), D_MODEL ** -0.5),
        "gmlp_ln_g": 1.0 + nrm(ks[3], (DEPTH, GMLP_GROUPS, HEAD_DIM), 0.02),
        "gmlp_ln_b": nrm(ks[4], (DEPTH, GMLP_GROUPS, HEAD_DIM), 0.02),
        "gmlp_ws": nrm(ks[5], (DEPTH, GMLP_GROUPS, GMLP_CHUNK, GMLP_CHUNK), GMLP_CHUNK ** -0.5),
        "gmlp_bs": 1.0 + nrm(ks[6], (DEPTH, GMLP_GROUPS, GMLP_CHUNK), 0.02),
        "lambda_q1": nrm(ks[7], (DEPTH, HEAD_DIM), 0.1),
        "lambda_k1": nrm(ks[8], (DEPTH, HEAD_DIM), 0.1),
        "lambda_q2": nrm(ks[9], (DEPTH, HEAD_DIM), 0.1),
        "lambda_k2": nrm(ks[10], (DEPTH, HEAD_DIM), 0.1),
        "conv_w": nrm(ks[11], (DEPTH, CONV_K, CONV_WIDTH), CONV_K ** -0.5),
        "mix_norm": 1.0 + nrm(ks[12], (DEPTH, MIX_WIDTH), 0.02),
        "w_out": nrm(ks[13], (DEPTH, MIX_WIDTH, D_MODEL), MIX_WIDTH ** -0.5),
        "ffn_norm": 1.0 + nrm(ks[14], (DEPTH, D_MODEL), 0.02),
        "w_gate": nrm(ks[15], (DEPTH, D_MODEL, D_FF), D_MODEL ** -0.5),
        "w_up": nrm(ks[16], (DEPTH, D_MODEL, D_FF), D_MODEL ** -0.5),
        "w_down": nrm(ks[17], (DEPTH, D_FF, D_MODEL), D_FF ** -0.5),
        "final_norm": 1.0 + nrm(ks[18], (D_MODEL,), 0.02),
    }


def reference(x, positions, attn_norm, w_in, gmlp_ln_g, gmlp_ln_b, gmlp_ws, gmlp_bs,
              lambda_q1, lambda_k1, lambda_q2, lambda_k2, conv_w, mix_norm, w_out,
              ffn_norm, w_gate, w_up, w_down, final_norm):
    b_, s_, _ = x.shape
    inv_freq = 1.0 / (ROPE_THETA ** (jnp.arange(0, HEAD_DIM, 2, dtype=jnp.float32) / HEAD_DIM))
    ang = positions.astype(jnp.float32)[..., None] * inv_freq
    cos = jnp.cos(ang)[:, :, None, None, :]
    sin = jnp.sin(ang)[:, :, None, None, :]

    for l in range(DEPTH):
        h = rms_norm(x, attn_norm[l])
        z = h @ w_in[l]
        za, zq, zk, zv, zc = jnp.split(z, SPLIT_IDX, axis=-1)

        ya = gmlp_mixer(za, gmlp_ln_g[l], gmlp_ln_b[l], gmlp_ws[l], gmlp_bs[l])

        lam_init = 0.8 - 0.6 * math.exp(-0.3 * l)
        lam = (jnp.exp(jnp.sum(lambda_q1[l].astype(jnp.float32) * lambda_k1[l].astype(jnp.float32)))
               - jnp.exp(jnp.sum(lambda_q2[l].astype(jnp.float32) * lambda_k2[l].astype(jnp.float32)))
               + lam_init)
        yb = diff_attention(zq.reshape(b_, s_, DIFF_HEADS, 2, HEAD_DIM),
                            zk.reshape(b_, s_, DIFF_HEADS, 2, HEAD_DIM),
                            zv.reshape(b_, s_, DIFF_HEADS, DIFF_V_DIM), lam, cos, sin)
        yb = group_rms(yb) * (1.0 - lam_init)

        bg, cg, hc = jnp.split(zc, 3, axis=-1)
        yc = short_conv_mixer(bg, cg, hc, conv_w[l]).reshape(b_, s_, CONV_GROUPS, HEAD_DIM)

        mix = jnp.concatenate([group_rms(ya).reshape(b_, s_, GMLP_WIDTH),
                               yb.reshape(b_, s_, DIFF_WIDTH),
                               group_rms(yc).reshape(b_, s_, CONV_WIDTH)], axis=-1)
        mix = mix * mix_norm[l].astype(mix.dtype)
        x = x + mix @ w_out[l]

        h = rms_norm(x, ffn_norm[l])
        x = x + (jax.nn.silu(h @ w_gate[l]) * (h @ w_up[l])) @ w_down[l]

    return rms_norm(x, final_norm)
```

```python
import math
import contextlib
import numpy as np
import ml_dtypes
import concourse.bass as bass
import concourse.mybir as mybir
from concourse.bass_utils import run_bass_kernel_spmd

F32 = mybir.dt.float32
BF16 = mybir.dt.bfloat16
I32 = mybir.dt.int32
AF = mybir.ActivationFunctionType
ALU = mybir.AluOpType
AX = mybir.AxisListType

D = 2048
NIN = 5632
DFF = 5632
T = 512
HD = 128
RMS_EPS = 1e-6
LN_EPS = 1e-5
NSLOT = 4
NKP = 3
NTF = 8
NTB = 5
TWO_PI = 2.0 * math.pi
CW1 = 6.28125
CW2 = TWO_PI - CW1
MAGIC = 12582912.0

SAME_ENGINE_SYNC = True
FAST_RECIP = False
ACT_RSQRT = True
POOL_OFFLOAD = True
ESUM = True


class SemObj:
    def __init__(self, h):
        self.h = h
        self.count = 0
        self.owner = None
        self.maxwait = 0


class Eng:
    def __init__(self, name, h, sem, fifo=False):
        self.name = name
        self.h = h
        self.sem = sem
        sem.owner = self
        self.known = {}
        self.fifo = fifo
        self.n = 0


class Buf:
    __slots__ = ("name", "w", "r", "dsem")

    def __init__(self, name):
        self.name = name
        self.w = None
        self.r = {}
        self.dsem = None


def _flat(xs):
    out = []
    for x in xs:
        if isinstance(x, (list, tuple)):
            out.extend(_flat(x))
        elif x is not None:
            out.append(x)
    return out


class K:
    def __init__(self):
        self.nc = bass.Bass("TRN2", target_bir_lowering=False)
        self.es = contextlib.ExitStack()
        self._sems = []
        nc = self.nc
        self.pe = Eng("pe", nc.tensor, self.newsem("pe"), fifo=True)
        self.act = Eng("act", nc.scalar, self.newsem("act"))
        self.dve = Eng("dve", nc.vector, self.newsem("dve"))
        self.pool = Eng("pool", nc.gpsimd, self.newsem("pool"))
        self.sp = Eng("sp", nc.sync, self.newsem("sp"))

    def newsem(self, name):
        s = SemObj(self.es.enter_context(self.nc.semaphore(name)))
        self._sems.append(s)
        return s

    def sbuf(self, name, shape, dt):
        return self.es.enter_context(self.nc.sbuf_tensor("s_" + name, shape, dt))

    def psum(self, name, shape, dt):
        return self.es.enter_context(self.nc.psum_tensor(name, shape, dt))

    def _wait(self, eng, deps):
        best = {}
        for (s, v) in deps:
            if best.get(s, 0) < v:
                best[s] = v
        for s, v in best.items():
            if s.owner is eng:
                if eng.fifo or not SAME_ENGINE_SYNC:
                    continue
                if v > s.count:
                    continue
            if eng.known.get(s, 0) >= v:
                continue
            eng.h.wait_ge(s.h, v)
            eng.known[s] = v
            if v > s.maxwait:
                s.maxwait = v

    def _deps(self, reads, writes):
        deps = []
        for b in reads:
            if b.w is not None:
                deps.append(b.w)
        for b in writes:
            if b.w is not None:
                deps.append(b.w)
            deps.extend(b.r.items())
        return deps

    def op(self, eng, fn, reads=(), writes=(), signal=True):
        reads = _flat(reads)
        writes = _flat(writes)
        self._wait(eng, self._deps(reads, writes))
        ins = fn()
        eng.n += 1
        if signal:
            ins.then_inc(eng.sem.h, 1)
            eng.sem.count += 1
            val = eng.sem.count
        else:
            val = eng.sem.count + 1
        rec = (eng.sem, val)
        for b in reads:
            if b.r.get(rec[0], 0) < val:
                b.r[rec[0]] = val
        for b in writes:
            b.w = rec
            b.r = {}
        return ins

    def dma(self, eng, out, in_, reads=(), writes=(), sem_buf=None, **kw):
        reads = _flat(reads)
        writes = _flat(writes)
        self._wait(eng, self._deps(reads, writes))
        sb = sem_buf if sem_buf is not None else writes[0]
        if sb.dsem is None:
            sb.dsem = self.newsem("d_" + sb.name)
        ins = eng.h.dma_start(out=out, in_=in_, **kw)
        ins.then_inc(sb.dsem.h, 16)
        sb.dsem.count += 16
        rec = (sb.dsem, sb.dsem.count)
        for b in reads:
            b.r[rec[0]] = rec[1]
        for b in writes:
            b.w = rec
            b.r = {}
        return ins

    def finish(self, final_bufs):
        deps = [b.w for b in final_bufs if b.w is not None]
        for s in self._sems:
            if s.owner is None and s.count > 0:
                deps.append((s, s.count))
        self._wait(self.sp, deps)
        for s in self._sems:
            assert s.maxwait <= s.count, ("wait beyond count", s.maxwait, s.count)
        self.es.close()


NCBF = 128 * 3 + 4 * 512 + 128
C_ID, C_SW, C_ON, C_MK, C_TR = 0, 128, 256, 384, 384 + 2048


def make_consts():
    cbf = np.zeros((128, NCBF), np.float32)
    cbf[:, C_ID:C_ID + 128] = np.eye(128)
    p = np.arange(128)
    sw = np.zeros((128, 128), np.float32)
    sw[(p + 64) % 128, p] = 1.0
    cbf[:, C_SW:C_SW + 128] = sw
    cbf[:, C_ON:C_ON + 128] = 1.0
    q = np.arange(512)
    for j in range(4):
        cbf[:, C_MK + j * 512:C_MK + (j + 1) * 512] = ((j * 128 + p)[:, None] <= q[None, :])
    cbf[:, C_TR:C_TR + 128] = (p[None, :] <= p[:, None])
    cf = np.zeros((128, 4), np.float32)
    inv_freq = (1.0 / (10000.0 ** (np.arange(0, 128, 2, dtype=np.float32) / np.float32(128)))).astype(np.float32)
    cf[:, 0] = inv_freq[p % 64]
    cf[:, 1] = np.where(p < 64, -1.0, 1.0)
    return cbf.astype(ml_dtypes.bfloat16), cf


def build(S, L, stop=99):
    NT = S // T
    k = K()
    nc = k.nc
    pe, act, dve, pool, sp = k.pe, k.act, k.dve, k.pool, k.sp

    def din(name, shape, dt):
        return nc.dram_tensor(name, shape, dt, kind="ExternalInput").ap()

    x_d = din("x", [S, D], F32)
    pos_d = din("pos", [1, S], I32)
    cbf_d = din("cbf", [128, NCBF], BF16)
    cf_d = din("cf", [128, 4], F32)
    gvec_d = din("gvec", [2 * L + 1, D], F32)
    lnv_d = din("lnv", [3 * L, 512], F32)
    ws_d = din("ws", [L, 4, 128, 128], F32)
    lam_d = din("lam", [L, 512], F32)
    cw_d = din("cw", [128, L * 12], F32)
    mn_d = din("mn", [128, L * 16], F32)
    win_d = din("win", [L, 11, 128, 8192], F32)
    wout_d = din("wout", [L, 4, 128, 8192], F32)
    wg_d = din("wg", [L, 11, 128, 8192], F32)
    wu_d = din("wu", [L, 11, 128, 8192], F32)
    wd_d = din("wd", [L, 11, 128, 8192], F32)
    out_d = nc.dram_tensor("out", [S, D], F32, kind="ExternalOutput").ap()
    kc_d = nc.dram_tensor("kcache", [L, 8, 128, S], BF16, kind="Internal").ap()
    vc_d = nc.dram_tensor("vcache", [L, S, 1024], BF16, kind="Internal").ap()
    cs_d = nc.dram_tensor("cstab", [2, 128, S], F32, kind="Internal").ap()
    b_kc = [Buf(f"kc{l}") for l in range(L)]
    b_vc = [Buf(f"vc{l}") for l in range(L)]
    b_csd = Buf("csd")
    b_out = Buf("out")

    xs = k.sbuf("xs", [128, 4, D], F32)
    b_x = [Buf(f"x{j}") for j in range(4)]
    hT = k.sbuf("hT", [128, 16, T], BF16)
    b_hT = [[Buf(f"hT{j}{h}") for h in range(2)] for j in range(4)]
    mixT = k.sbuf("mixT", [128, 16, T], BF16)
    b_mix = [Buf(f"mix{c}") for c in range(16)]
    wsl = [k.sbuf(f"wsl{i}", [128, 8192], BF16) for i in range(NSLOT)]
    b_wsl = [Buf(f"wsl{i}") for i in range(NSLOT)]
    arena = k.sbuf("arena", [128, 12288], BF16)
    b_ar = [Buf(f"ar{i}") for i in range(12)]
    kp = [k.sbuf(f"kp{i}", [128, T], BF16) for i in range(NKP)]
    b_kp = [Buf(f"kp{i}") for i in range(NKP)]
    vp = [k.sbuf(f"vp{i}", [128, 4, 256], BF16) for i in range(NKP)]
    b_vp = [Buf(f"vp{i}") for i in range(NKP)]
    tfs = [k.sbuf(f"tf{i}", [128, T], F32) for i in range(NTF)]
    b_tf = [Buf(f"tf{i}") for i in range(NTF)]
    tbs = [k.sbuf(f"tb{i}", [128, T], BF16) for i in range(NTB)]
    b_tb = [Buf(f"tb{i}") for i in range(NTB)]
    cbf = k.sbuf("cbf", [128, NCBF], BF16)
    b_cbf = Buf("cbf")
    cf = k.sbuf("cf", [128, 4], F32)
    b_cf = Buf("cf")
    lnv = k.sbuf("lnv", [128, 3, 512], F32)
    b_lnv = Buf("lnv")
    wsT = k.sbuf("wsT", [128, L, 4, 128], BF16)
    b_wsT = Buf("wsT")
    cw = k.sbuf("cw", [128, L * 12], F32)
    b_cw = Buf("cw")
    mn = k.sbuf("mn", [128, L * 16], F32)
    b_mn = Buf("mn")
    nlam = k.sbuf("nlam", [128, L], F32)
    b_nlam = Buf("nlam")
    st = k.sbuf("st", [128, 64], F32)
    b_ssq = [Buf(f"ssq{j}") for j in range(4)]
    b_rstd = [Buf(f"rstd{j}") for j in range(4)]
    b_bn = [Buf(f"bn{j}") for j in range(4)]
    carry = k.sbuf("carry", [128, L, 4, 2], F32)
    b_carry = [[Buf(f"carry{l}{g}") for g in range(4)] for l in range(L)]
    xh = [k.sbuf(f"xh{i}", [128, T + 2], F32) for i in range(2)]
    b_xh = [Buf(f"xh{i}") for i in range(2)]

    o1a = k.sbuf("o1a", [128, T], F32)
    o1b = k.sbuf("o1b", [128, T], F32)
    b_o1 = [Buf("o1a"), Buf("o1b")]
    cs_tile = k.sbuf("cs_tile", [128, T], F32)
    b_cs = Buf("cs_tile")
    sn_tile = k.sbuf("sn_tile", [128, T], F32)
    b_sn = Buf("sn_tile")
    banks = [k.psum(f"bank{i}", [128, T], F32) for i in range(8)]
    b_bank = [Buf(f"bank{i}") for i in range(8)]

    def ar_bf(c0, n):
        return arena[:, c0:c0 + n]

    def ar_f32(c0, n):
        return arena[:, c0:c0 + n].bitcast(F32)

    gbc = ar_f32(0, 4096)
    b_gbc = b_ar[0:4]
    hnb = [ar_bf(4096, 2048), ar_bf(6144, 2048)]
    b_hn = [b_ar[4:6], b_ar[6:8]]
    actT = [ar_bf(8192, 2048).rearrange("p (c t) -> p c t", c=4), ar_bf(10240, 2048).rearrange("p (c t) -> p c t", c=4)]
    b_actT = [b_ar[8:10], b_ar[10:12]]
    b_actTc = [[Buf(f"actT{i}{c}") for c in range(4)] for i in range(2)]
    uT = ar_bf(0, 2048).rearrange("p (g t) -> p g t", g=4)
    b_uT = [b_ar[g // 2] for g in range(4)]
    bgT = ar_bf(2048, 2048).rearrange("p (g t) -> p g t", g=4)
    b_bg = [b_ar[2 + g // 2] for g in range(4)]
    vn = ar_bf(4096, 2048).rearrange("p (j c) -> p j c", j=4)
    b_vn = b_ar[4:6]
    cgT = ar_bf(6144, 2048).rearrange("p (g t) -> p g t", g=4)
    b_cg = [b_ar[6 + g // 2] for g in range(4)]
    qT = ar_bf(0, 4096).rearrange("p (c t) -> p c t", c=8)
    b_qT = [b_ar[c // 2] for c in range(8)]
    kTc = ar_bf(4096, 4096).rearrange("p (c t) -> p c t", c=8)
    b_kT = [b_ar[4 + c // 2] for c in range(8)]
    Vc = ar_bf(8192, 4096).rearrange("p (j c) -> p j c", j=4)
    b_V = [b_ar[8 + j] for j in range(4)]

    ident = cbf[:, C_ID:C_ID + 128]
    swap = cbf[:, C_SW:C_SW + 128]
    ones = cbf[:, C_ON:C_ON + 128]
    tril = cbf[:, C_TR:C_TR + 128]

    def mask(j):
        return cbf[:, C_MK + j * 512:C_MK + (j + 1) * 512]

    state = {"tf": 0, "tb": 0, "bank": 0, "bankset": list(range(8)), "kp": 0}

    def tf():
        i = state["tf"]
        state["tf"] = (i + 1) % NTF
        return tfs[i], b_tf[i]

    def tb():
        i = state["tb"]
        state["tb"] = (i + 1) % NTB
        return tbs[i], b_tb[i]

    def bank():
        bs = state["bankset"]
        i = bs[state["bank"] % len(bs)]
        state["bank"] += 1
        return banks[i], b_bank[i]

    def ACT(out, in_, func, reads, writes, **kw):
        return k.op(act, lambda: nc.scalar.activation(out=out, in_=in_, func=func, **kw), reads, writes)

    def TT(out, in0, in1, op, reads, writes, eng=None):
        e = eng or dve
        return k.op(e, lambda: e.h.tensor_tensor(out=out, in0=in0, in1=in1, op=op), reads, writes)

    def TS(out, in0, s1, s2, op0, op1, reads, writes):
        if op1 is None:
            return k.op(dve, lambda: nc.vector.tensor_scalar(out=out, in0=in0, scalar1=s1, scalar2=None, op0=op0), reads, writes)
        return k.op(dve, lambda: nc.vector.tensor_scalar(out=out, in0=in0, scalar1=s1, scalar2=s2, op0=op0, op1=op1), reads, writes)

    def STT(out, in0, scalar, in1, op0, op1, reads, writes):
        return k.op(dve, lambda: nc.vector.scalar_tensor_tensor(out=out, in0=in0, scalar=scalar, in1=in1, op0=op0, op1=op1), reads, writes)

    def CP(eng, out, in_, reads, writes):
        if eng is act:
            return ACT(out, in_, AF.Copy, reads, writes)
        return k.op(eng, lambda: eng.h.tensor_copy(out=out, in_=in_), reads, writes)

    def RECIP(out, in_, reads, writes):
        return k.op(dve, lambda: nc.vector.reciprocal(out=out, in_=in_), reads, writes)

    def MM(out, lhsT, rhs, start, stop, reads, writes, signal):
        return k.op(pe, lambda: nc.tensor.matmul(out, lhsT, rhs, start=start, stop=stop), reads, writes, signal=signal)

    wlist = []
    for t in range(NT):
        for l in range(L):
            for b in (0, 1, 8, 9, 10, 2, 3, 4, 5, 6, 7):
                wlist.append(win_d[l, b])
            for b in range(4):
                wlist.append(wout_d[l, b])
            for fg in range(11):
                wlist.append(wg_d[l, fg])
                wlist.append(wu_d[l, fg])
                wlist.append(wd_d[l, fg])
    wstate = {"issued": 0, "next": 0, "done": 0}

    def w_prefetch():
        while wstate["issued"] < min(wstate["done"] + NSLOT, len(wlist)):
            n = wstate["issued"]
            s = n % NSLOT
            k.dma(pool, wsl[s][:, :], wlist[n], writes=[b_wsl[s]], max_dma_last_dim=8192)
            wstate["issued"] += 1

    def w_next():
        n = wstate["next"]
        wstate["next"] += 1
        assert n < wstate["issued"], "weight block not issued"
        s = n % NSLOT
        return wsl[s], b_wsl[s]

    def w_done(cnt=1):
        wstate["done"] += cnt
        w_prefetch()

    k.dma(sp, cbf[:, :], cbf_d[:, :], writes=[b_cbf])
    k.dma(sp, cf[:, :], cf_d[:, :], writes=[b_cf])
    k.dma(sp, cw[:, :], cw_d[:, :], writes=[b_cw])
    k.dma(sp, mn[:, :], mn_d[:, :], writes=[b_mn])
    w_prefetch()
    k.op(dve, lambda: nc.vector.memset(carry[:, :, :, :], 0.0), [], [b_carry])

    for t in range(NT):
        sl = slice(t * T, (t + 1) * T)
        pi_t, pi_b = tf()
        pos_i = pi_t[:, :].bitcast(I32)
        k.dma(sp, pos_i, pos_d[0:1, sl].partition_broadcast(128), writes=[pi_b])
        pf_t, pf_b = tf()
        CP(dve, pf_t[:, :], pos_i, [pi_b], [pf_b])
        ang_t, ang_b = cs_tile, b_cs
        TS(ang_t[:, :], pf_t[:, :], cf[:, 0:1], None, ALU.mult, None, [pf_b, b_cf], [ang_b])
        for which in (1, 0):
            y_t, y_b = tf()
            if which == 0:
                TS(y_t[:, :], ang_t[:, :], math.pi / 2.0, None, ALU.add, None, [ang_b], [y_b])
                src, src_b = y_t, y_b
            else:
                src, src_b = ang_t, ang_b
            kf_t, kf_b = tf()
            TS(kf_t[:, :], src[:, :], 1.0 / TWO_PI, MAGIC, ALU.mult, ALU.add, [src_b], [kf_b])
            TS(kf_t[:, :], kf_t[:, :], -MAGIC, None, ALU.add, None, [kf_b], [kf_b])
            r_t, r_b = tf()
            STT(r_t[:, :], kf_t[:, :], -CW1, src[:, :], ALU.mult, ALU.add, [kf_b, src_b], [r_b])
            STT(r_t[:, :], kf_t[:, :], -CW2, r_t[:, :], ALU.mult, ALU.add, [kf_b, r_b], [r_b])
            TS(r_t[:, :], r_t[:, :], -math.pi, math.pi, ALU.max, ALU.min, [r_b], [r_b])
            ACT(r_t[:, :], r_t[:, :], AF.Sin, [r_b], [r_b])
            if which == 1:
                TS(r_t[:, :], r_t[:, :], cf[:, 1:2], None, ALU.mult, None, [r_b, b_cf], [r_b])
            k.dma(sp, cs_d[which, :, sl], r_t[:, :], reads=[r_b], writes=[b_csd])

    for l in range(L if stop > 0 else 0):
        lv_t, lv_b = tf()
        k.dma(sp, lv_t[:, :], lam_d[l:l + 1, :].partition_broadcast(128), writes=[lv_b])
        pr_t, pr_b = tf()
        TT(pr_t[:, 0:128], lv_t[:, 0:128], lv_t[:, 128:256], ALU.mult, [lv_b], [pr_b])
        TT(pr_t[:, 128:256], lv_t[:, 256:384], lv_t[:, 384:512], ALU.mult, [lv_b], [pr_b])
        k.op(dve, lambda: nc.vector.reduce_sum(out=st[:, 32:33], in_=pr_t[:, 0:128], axis=AX.X), [pr_b], [b_bn[0]])
        k.op(dve, lambda: nc.vector.reduce_sum(out=st[:, 33:34], in_=pr_t[:, 128:256], axis=AX.X), [pr_b], [b_bn[0]])
        ACT(st[:, 34:36], st[:, 32:34], AF.Exp, [b_bn[0]], [b_bn[1]])
        TT(st[:, 36:37], st[:, 35:36], st[:, 34:35], ALU.subtract, [b_bn[1]], [b_bn[2]])
        lam_init = 0.8 - 0.6 * math.exp(-0.3 * l)
        TS(nlam[:, l:l + 1], st[:, 36:37], -lam_init, None, ALU.add, None, [b_bn[2]], [b_nlam])
        TS(mn[:, l * 16 + 4:l * 16 + 12], mn[:, l * 16 + 4:l * 16 + 12], 1.0 - lam_init, None, ALU.mult, None, [b_mn], [b_mn])
        for g in range(4):
            w_t, w_b = tf()
            k.dma(sp, w_t[:, 0:128], ws_d[l, g], writes=[w_b])
            wm_t, wm_b = tb()
            TT(wm_t[:, 0:128], w_t[:, 0:128], tril, ALU.mult, [w_b, b_cbf], [wm_b])
            bk, bk_b = bank()
            pT = bk[:, :].bitcast(BF16)
            k.op(pe, lambda: nc.tensor.transpose(pT[:, 0:128], wm_t[:, 0:128], ident), [wm_b, b_cbf], [bk_b])
            CP(dve, wsT[:, l, g, :], pT[:, 0:128], [bk_b], [b_wsT])

    def STTe(eng, out, in0, scalar, in1, op0, op1, reads, writes):
        return k.op(eng, lambda: eng.h.scalar_tensor_tensor(out=out, in0=in0, scalar=scalar, in1=in1, op0=op0, op1=op1), reads, writes)

    junk = ar_bf(10240, 2048)
    b_junk = b_ar[10:12]

    def norm_pre(j, transposing=True):
        ACT(junk, xs[:, j, :], AF.Square, [b_x[j]], [b_junk, b_ssq[j]], accum_out=st[:, j:j + 1])
        ACT(st[:, 4 + j:5 + j], st[:, j:j + 1], AF.Ln, [b_ssq[j]], [b_rstd[j]], scale=1.0 / D, bias=RMS_EPS)
        ACT(st[:, 4 + j:5 + j], st[:, 4 + j:5 + j], AF.Exp, [b_rstd[j]], [b_rstd[j]], scale=-0.5)
        if transposing and j < 2:
            STTe(dve, hnb[j], xs[:, j, :], st[:, 4 + j:5 + j], gbc, ALU.mult, ALU.mult,
                 [b_x[j], b_rstd[j], b_gbc], [b_hn[j]])

    gbc_fin = ar_f32(4096, 4096)
    b_gbcf = b_ar[4:8]

    def load_gbc(gi):
        k.dma(sp, gbc, gvec_d[gi:gi + 1, :].partition_broadcast(128), writes=[b_gbc])

    def load_gbc_fin():
        k.dma(sp, gbc_fin, gvec_d[2 * L:2 * L + 1, :].partition_broadcast(128), writes=[b_gbcf])

    def norm_stage(gi):
        for j in range(4):
            hn, hn_b = hnb[j % 2], b_hn[j % 2]
            if j >= 2:
                STTe(dve, hn, xs[:, j, :], st[:, 4 + j:5 + j], gbc, ALU.mult, ALU.mult,
                     [b_x[j], b_rstd[j], b_gbc], [hn_b])
            for half in range(2):
                bk, bk_b = bank()
                pT = bk[:, :].bitcast(BF16).rearrange("p (c t) -> p c t", c=8)
                for c in range(8):
                    cc = half * 8 + c
                    k.op(pe, lambda: nc.tensor.transpose(pT[:, c, :], hn[:, cc * 128:(cc + 1) * 128], ident),
                         [hn_b, b_cbf], [bk_b] if c in (0, 7) else [], signal=(c == 7))
                eng = act if half == 0 else dve
                CP(eng, hT[:, half * 8:(half + 1) * 8, j * 128:(j + 1) * 128], pT[:, :, :], [bk_b], [b_hT[j][half]])

    def mm_feat(w, c0, srcT, src_bufs, w_b):
        bk, bk_b = bank()
        wv = w[:, :].rearrange("p (k c) -> p k c", k=16)
        for kc in range(16):
            MM(bk[:, :], wv[:, kc, c0:c0 + 128], srcT[:, kc, :], kc == 0, kc == 15,
               [w_b, src_bufs], [bk_b] if kc in (0, 15) else [], kc == 15)
        return bk, bk_b

    def mm_tok(w, c0, ncols, srcT, src_bufs, j, w_b, order=None):
        bk, bk_b = bank()
        wv = w[:, :].rearrange("p (k c) -> p k c", k=16)
        order = order or list(range(16))
        for i, kc in enumerate(order):
            sb_ = src_bufs[kc] if (order is not None and len(src_bufs) == 16) else src_bufs
            MM(bk[:, 0:ncols], srcT[:, kc, j * 128:(j + 1) * 128], wv[:, kc, c0:c0 + ncols], i == 0, i == 15,
               [w_b, sb_], [bk_b] if i in (0, 15) else [], i == 15)
        return bk, bk_b

    def tail(ys, y_bufs, chunk_ids, l):
        n = 128 * len(ys)
        bk, bk_b = bank()
        for i, (y, yb) in enumerate(zip(ys, y_bufs)):
            sq, sq_b = tb()
            ACT(sq[:, :], y, AF.Square, [yb], [sq_b])
            MM(bk[:, :], ones, sq[:, :], i == 0, i == len(ys) - 1, [sq_b, b_cbf], [bk_b], True)
        rt, rt_b = tf()
        ACT(rt[:, :], bk[:, :], AF.Sqrt, [bk_b], [rt_b], scale=1.0 / n, bias=RMS_EPS)
        RECIP(rt[:, :], rt[:, :], [rt_b], [rt_b])
        for y, yb, c in zip(ys, y_bufs, chunk_ids):
            STT(mixT[:, c, :], y, mn[:, l * 16 + c:l * 16 + c + 1], rt[:, :], ALU.mult, ALU.mult,
                [yb, rt_b, b_mn], [b_mix[c]])

    all_hT = [b_hT[j][h] for j in range(4) for h in range(2)]

    free_tf = list(range(NTF))
    free_tb = list(range(NTB))

    def tfa():
        assert free_tf, "out of fp32 temps"
        i = free_tf.pop(0)
        return tfs[i], b_tf[i], i

    def tff(i):
        free_tf.append(i)

    def tba():
        assert free_tb, "out of bf16 temps"
        i = free_tb.pop(0)
        return tbs[i], b_tb[i], i

    def tbf(i):
        free_tb.append(i)

    chains = []

    def spawn(gen):
        try:
            w = next(gen)
            chains.append([gen, w])
        except StopIteration:
            pass

    def tick():
        for ch in list(chains):
            ch[1] -= 1
            if ch[1] <= 0:
                try:
                    ch[1] = next(ch[0])
                except StopIteration:
                    chains.remove(ch)

    def drain():
        while chains:
            tick()

    def RECIPF(out, in_, reads, writes):
        if FAST_RECIP:
            return k.op(dve, lambda: nc.vector.reciprocal_approx_fast(out=out, in_=in_), reads, writes)
        return RECIP(out, in_, reads, writes)

    def tail_chain(ys, y_bufs, y_ids, chunk_ids, l, wait, pre=0, sq_on_dve=False):
        n = 128 * len(ys)
        if pre:
            yield pre
        sqs = []
        for y, yb in zip(ys, y_bufs):
            sq, sq_b, sq_i = tba()
            if sq_on_dve:
                TT(sq[:, :], y, y, ALU.mult, [yb], [sq_b])
            else:
                ACT(sq[:, :], y, AF.Square, [yb], [sq_b])
            sqs.append((sq, sq_b, sq_i))
        yield wait
        bk, bk_b = bank()
        for i, (sq, sq_b, sq_i) in enumerate(sqs):
            MM(bk[:, :], ones, sq[:, :], i == 0, i == len(sqs) - 1, [sq_b, b_cbf], [bk_b], True)
            tbf(sq_i)
        rt, rt_b, rt_i = tfa()
        if ACT_RSQRT:
            ACT(rt[:, :], bk[:, :], AF.Ln, [bk_b], [rt_b], scale=1.0 / n, bias=RMS_EPS)
            ACT(rt[:, :], rt[:, :], AF.Exp, [rt_b], [rt_b], scale=-0.5)
        else:
            ACT(rt[:, :], bk[:, :], AF.Sqrt, [bk_b], [rt_b], scale=1.0 / n, bias=RMS_EPS)
            RECIPF(rt[:, :], rt[:, :], [rt_b], [rt_b])
        for y, yb, yi, c in zip(ys, y_bufs, y_ids, chunk_ids):
            STT(mixT[:, c, :], y, mn[:, l * 16 + c:l * 16 + c + 1], rt[:, :], ALU.mult, ALU.mult,
                [yb, rt_b, b_mn], [b_mix[c]])
            if yi is not None:
                tff(yi)
        tff(rt_i)

    def mmf(w, c0, srcT, src_bufs, w_b):
        tick()
        return mm_feat(w, c0, srcT, src_bufs, w_b)

    def mmt(w, c0, ncols, srcT, src_bufs, j, w_b):
        tick()
        return mm_tok(w, c0, ncols, srcT, src_bufs, j, w_b)

    def layer_tile(t, l):
        sl = slice(t * T, (t + 1) * T)
        state["bankset"] = list(range(8))
        norm_stage(2 * l)
        k.dma(sp, lnv[:, :, :], lnv_d[3 * l:3 * l + 3, :].partition_broadcast(128), writes=[b_lnv])
        k.dma(sp, cs_tile[:, :], cs_d[0, :, sl], reads=[b_csd], writes=[b_cs])
        k.dma(sp, sn_tile[:, :], cs_d[1, :, sl], reads=[b_csd], writes=[b_sn])
        w, w_b = w_next()
        for g in range(4):
            bk, bk_b = mmf(w, g * 128, hT, all_hT, w_b)
            ACT(uT[:, g, :], bk[:, :], AF.Gelu, [bk_b], [b_uT[g]])
        w_done()
        w, w_b = w_next()
        vgs = []
        for j in range(4):
            bk, bk_b = mmt(w, 0, 512, hT, all_hT, j, w_b)
            vg, vg_b, vg_i = tfa()
            ACT(vg[:, :], bk[:, :], AF.Gelu, [bk_b], [vg_b])
            vgs.append((vg, vg_b, vg_i))
        w_done()
        sc, sc_b, sc_i = tfa()
        for j in range(4):
            vg, vg_b, vg_i = vgs[j]
            for g in range(4):
                o = (j * 4 + g) * 6
                k.op(dve, lambda: nc.vector.bn_stats(out=sc[:, o:o + 6], in_=vg[:, g * 128:(g + 1) * 128]), [vg_b], [sc_b])
            for g in range(4):
                o = (j * 4 + g) * 6
                m = 96 + (j * 4 + g) * 2
                k.op(dve, lambda: nc.vector.bn_aggr(out=sc[:, m:m + 2], in_=sc[:, o:o + 6]), [sc_b], [sc_b])
        var16 = sc[:, 96:128].rearrange("p (g two) -> p g two", two=2)[:, :, 1]
        ACT(sc[:, 128:144], var16, AF.Ln, [sc_b], [sc_b], bias=LN_EPS)
        ACT(sc[:, 128:144], sc[:, 128:144], AF.Exp, [sc_b], [sc_b], scale=-0.5)
        for j in range(4):
            vg, vg_b, vg_i = vgs[j]
            for g in range(4):
                m = 96 + (j * 4 + g) * 2
                r_ = 128 + j * 4 + g
                TS(vg[:, g * 128:(g + 1) * 128], vg[:, g * 128:(g + 1) * 128], sc[:, m:m + 1], sc[:, r_:r_ + 1],
                   ALU.subtract, ALU.mult, [vg_b, sc_b], [vg_b])
            TT(vg[:, :], vg[:, :], lnv[:, 0, :], ALU.mult, [vg_b, b_lnv], [vg_b])
            TT(vn[:, j, :], vg[:, :], lnv[:, 1, :], ALU.add, [vg_b, b_lnv], [b_vn])
            tff(vg_i)
        tff(sc_i)

        def chainA():
            yield 6
            for g in range(4):
                bk, bk_b = bank()
                for j in range(4):
                    MM(bk[:, j * 128:(j + 1) * 128], vn[:, j, g * 128:(g + 1) * 128], wsT[:, l, g, :], True, True,
                       [b_vn, b_wsT], [bk_b], j == 3)
                ya, ya_b, ya_i = tfa()
                for j in range(4):
                    TT(ya[:, j * 128:(j + 1) * 128], bk[:, j * 128:(j + 1) * 128], lnv[:, 2, g * 128:(g + 1) * 128], ALU.add,
                       [bk_b, b_lnv], [ya_b])
                TT(ya[:, :], ya[:, :], uT[:, g, :], ALU.mult, [ya_b, b_uT[g]], [ya_b], eng=(pool if POOL_OFFLOAD else dve))
                spawn(tail_chain([ya[:, :]], [ya_b], [ya_i], [g], l, 2))
                yield 1

        spawn(chainA())
        w, w_b = w_next()
        for g in range(4):
            bk, bk_b = mmf(w, g * 128, hT, all_hT, w_b)
            CP(act, bgT[:, g, :], bk[:, :], [bk_b], [b_bg[g]])
        w_done()
        w, w_b = w_next()
        for g in range(4):
            bk, bk_b = mmf(w, g * 128, hT, all_hT, w_b)
            CP(act, cgT[:, g, :], bk[:, :], [bk_b], [b_cg[g]])
        w_done()
        w, w_b = w_next()
        for g in range(4):
            bk, bk_b = mmf(w, g * 128, hT, all_hT, w_b)
            xe, xe_b = xh[g % 2], b_xh[g % 2]
            CP(dve, xe[:, 0:2], carry[:, l, g, :], [b_carry[l][g]], [xe_b])
            TT(xe[:, 2:T + 2], bk[:, :], cgT[:, g, :], ALU.mult, [bk_b, b_cg[g]], [xe_b])
            CP(dve, carry[:, l, g, :], xe[:, T:T + 2], [xe_b], [b_carry[l][g]])
            y, y_b, y_i = tfa()
            cwb = l * 12 + g * 3
            ce = pool if POOL_OFFLOAD else dve
            TS(y[:, :], xe[:, 2:T + 2], cw[:, cwb + 2:cwb + 3], None, ALU.mult, None, [xe_b, b_cw], [y_b])
            STT(y[:, :], xe[:, 1:T + 1], cw[:, cwb + 1:cwb + 2], y[:, :], ALU.mult, ALU.add, [xe_b, b_cw, y_b], [y_b])
            STT(y[:, :], xe[:, 0:T], cw[:, cwb:cwb + 1], y[:, :], ALU.mult, ALU.add, [xe_b, b_cw, y_b], [y_b])
            TT(y[:, :], y[:, :], bgT[:, g, :], ALU.mult, [y_b, b_bg[g]], [y_b], eng=ce)
            spawn(tail_chain([y[:, :]], [y_b], [y_i], [12 + g], l, 3))
        w_done()
        cs_t, cs_b, sn_t, sn_b = cs_tile, b_cs, sn_tile, b_sn

        def rope_chain(bk, bk_b, dst, dst_b):
            raw, raw_b, raw_i = tba()
            CP(act, raw[:, :], bk[:, :], [bk_b], [raw_b])
            t1, t1_b, t1_i = tfa()
            TT(t1[:, :], bk[:, :], cs_t[:, :], ALU.mult, [bk_b, cs_b, raw_b], [t1_b])
            yield 2
            bk2, bk2_b = bank()
            MM(bk2[:, :], swap, raw[:, :], True, True, [raw_b, b_cbf], [bk2_b], True)
            tbf(raw_i)
            t2, t2_b, t2_i = tfa()
            TT(t2[:, :], bk2[:, :], sn_t[:, :], ALU.mult, [bk2_b, sn_b], [t2_b])
            TT(dst, t1[:, :], t2[:, :], ALU.add, [t1_b, t2_b], [dst_b], eng=(pool if POOL_OFFLOAD else dve))
            tff(t1_i)
            tff(t2_i)

        for blk in range(4):
            w, w_b = w_next()
            for cc in range(4):
                c = (blk % 2) * 4 + cc
                bk, bk_b = mmf(w, cc * 128, hT, all_hT, w_b)
                if blk < 2:
                    spawn(rope_chain(bk, bk_b, qT[:, c, :], b_qT[c]))
                else:
                    spawn(rope_chain(bk, bk_b, kTc[:, c, :], b_kT[c]))
            w_done()
        for blk in range(2):
            w, w_b = w_next()
            for j in range(4):
                bk, bk_b = mmt(w, 0, 512, hT, all_hT, j, w_b)
                CP(act, Vc[:, j, blk * 512:(blk + 1) * 512], bk[:, :], [bk_b], [b_V[j]])
            w_done()
        drain()
        state["bankset"] = [0, 1]
        state["bank"] = 0
        scale = 1.0 / math.sqrt(HD)
        accsets = [[(banks[2 + e], b_bank[2 + e]) for e in range(3)], [(banks[5 + e], b_bank[5 + e]) for e in range(3)]]
        pieces = [(h, jj, p) for h in range(4) for jj in range(2) for p in range(t + 1)]
        loaded = {}
        lstate = {"n": 0}

        def load_piece(idx):
            if idx >= len(pieces) or idx in loaded:
                return
            h, jj, p = pieces[idx]
            if p == t:
                loaded[idx] = None
                return
            i = lstate["n"] % NKP
            lstate["n"] += 1
            psl = slice(p * T, (p + 1) * T)
            k.dma(sp, kp[i][:, :], kc_d[l, 2 * h + jj, :, psl], reads=[b_kc[l]], writes=[b_kp[i]])
            k.dma(sp, vp[i][:, :, :], vc_d[l, psl, h * 256:(h + 1) * 256].rearrange("(j p) c -> p j c", p=128),
                  reads=[b_vc[l]], writes=[b_vp[i]])
            loaded[idx] = i

        blocks = [(pi, b) for pi in range(len(pieces)) for b in range(4)]
        Sinfo = {}
        esum = {}

        def emit_S(bi):
            pi, b = blocks[bi]
            h, jj, p = pieces[pi]
            ch = 2 * h + jj
            if b == 0:
                load_piece(pi)
                load_piece(pi + 1)
            if b == 2:
                load_piece(pi + 2)
            if p < t:
                i = loaded[pi]
                Kb, Kb_b, c0 = kp[i][:, b * 128:(b + 1) * 128], b_kp[i], 0
            else:
                Kb, Kb_b, c0 = kTc[:, ch, b * 128:(b + 1) * 128], b_kT[ch], b * 128
            sb, sb_b = bank()
            MM(sb[:, c0:T], Kb, qT[:, ch, c0:T], True, True, [Kb_b, b_qT[ch]], [sb_b], True)
            E, E_b, E_i = tba()
            ACT(E[:, c0:T], sb[:, c0:T], AF.Exp, [sb_b], [E_b], scale=scale)
            if p == t:
                TT(E[:, c0:T], E[:, c0:T], mask(b)[:, c0:T], ALU.mult, [E_b, b_cbf], [E_b])
            if ESUM:
                if p == 0 and b == 0:
                    Es, Es_b, Es_i = tfa()
                    esum[(h, jj)] = (Es, Es_b, Es_i)
                    CP(pool, Es[:, :], E[:, :], [E_b], [Es_b])
                else:
                    Es, Es_b, Es_i = esum[(h, jj)]
                    TT(Es[:, c0:T], Es[:, c0:T], E[:, c0:T], ALU.add, [Es_b, E_b], [Es_b], eng=pool)
            Sinfo[bi] = (E, E_b, E_i, c0)

        o1 = [o1a, o1b]

        def emit_AV(bi):
            pi, b = blocks[bi]
            h, jj, p = pieces[pi]
            E, E_b, E_i, c0 = Sinfo.pop(bi)
            acc = accsets[jj]
            if p < t:
                i = loaded[pi]
                Vb, Vb_b = vp[i][:, b, :], b_vp[i]
            else:
                Vb, Vb_b = Vc[:, b, h * 256:(h + 1) * 256], b_V[b]
            first = (p == 0 and b == 0)
            last = (p == t and b == 3)
            MM(acc[0][0][:, c0:T], Vb[:, 0:128], E[:, c0:T], first, last, [Vb_b, E_b], [acc[0][1]] if (first or last) else [], False)
            MM(acc[1][0][:, c0:T], Vb[:, 128:256], E[:, c0:T], first, last, [Vb_b, E_b], [acc[1][1]] if (first or last) else [], ESUM)
            if not ESUM:
                MM(acc[2][0][:, c0:T], ones, E[:, c0:T], first, last, [E_b, b_cbf], [acc[2][1]] if (first or last) else [], True)
            tbf(E_i)
            if ESUM and last:
                Es, Es_b, Es_i = esum.pop((h, jj))
                Esb, Esb_b, Esb_i = tba()
                CP(act, Esb[:, :], Es[:, :], [Es_b], [Esb_b])
                MM(acc[2][0][:, :], ones, Esb[:, :], True, True, [Esb_b, b_cbf], [acc[2][1]], True)
                tbf(Esb_i)
                tff(Es_i)
            if last:
                r, r_b, r_i = tfa()
                if ACT_RSQRT:
                    ACT(r[:, :], acc[2][0][:, :], AF.Ln, [acc[2][1]], [r_b])
                    ACT(r[:, :], r[:, :], AF.Exp, [r_b], [r_b], scale=-1.0)
                else:
                    RECIPF(r[:, :], acc[2][0][:, :], [acc[2][1]], [r_b])
                if jj == 0:
                    for e in range(2):
                        TT(o1[e][:, :], acc[e][0][:, :], r[:, :], ALU.mult, [acc[e][1], r_b], [b_o1[e]])
                    tff(r_i)
                else:
                    for e in range(2):
                        bb, bb_b, bb_i = tfa()
                        TT(bb[:, :], acc[e][0][:, :], r[:, :], ALU.mult, [acc[e][1], r_b], [bb_b])
                        STT(o1[e][:, :], bb[:, :], nlam[:, l:l + 1], o1[e][:, :], ALU.mult, ALU.add, [bb_b, b_o1[e], b_nlam], [b_o1[e]])
                        tff(bb_i)
                    tff(r_i)
                    spawn(tail_chain([o1[0][:, :], o1[1][:, :]], [b_o1[0], b_o1[1]], [None, None], [4 + 2 * h, 5 + 2 * h], l, (2 if t == 0 else 3), pre=(1 if t == 0 else 4), sq_on_dve=True))

        nblk = len(blocks)
        emit_S(0)
        for bi in range(nblk):
            if bi + 1 < nblk:
                emit_S(bi + 1)
            emit_AV(bi)
            tick()
        state["bankset"] = list(range(8))
        load_gbc(2 * l + 1)
        if t < NT - 1:
            k.dma(sp, kc_d[l, :, :, sl].rearrange("c d t -> d c t"), kTc[:, :, :], reads=[b_kT], writes=[b_kc[l]])
            k.dma(sp, vc_d[l, sl, :].rearrange("(j p) c -> p j c", p=128), Vc[:, :, :], reads=[b_V], writes=[b_vc[l]])
        oorder = [0, 1, 2, 3, 12, 13, 14, 15, 4, 5, 6, 7, 8, 9, 10, 11]
        for nb in range(4):
            w, w_b = w_next()
            wv = w[:, :].rearrange("p (k c) -> p k c", k=16)

            def omm(bk, bk_b, j, i):
                kc = oorder[i]
                MM(bk[:, :], mixT[:, kc, j * 128:(j + 1) * 128], wv[:, kc, :], i == 0, i == 15,
                   [w_b, b_mix[kc]], [bk_b] if i in (0, 15) else [], i == 15)

            def oadd(bk, bk_b, j):
                TT(xs[:, j, nb * 512:(nb + 1) * 512], xs[:, j, nb * 512:(nb + 1) * 512], bk[:, :], ALU.add,
                   [b_x[j], bk_b], [b_x[j]])
                if nb == 3:
                    norm_pre(j, True)

            if nb == 0:
                grp = []
                for j in range(4):
                    bk, bk_b = bank()
                    grp.append((bk, bk_b))
                    for i in range(14):
                        omm(bk, bk_b, j, i)
                    tick()
                drain()
                for j in range(4):
                    bk, bk_b = grp[j]
                    omm(bk, bk_b, j, 14)
                    omm(bk, bk_b, j, 15)
                    oadd(bk, bk_b, j)
            else:
                for j in range(4):
                    bk, bk_b = bank()
                    for i in range(16):
                        omm(bk, bk_b, j, i)
                    oadd(bk, bk_b, j)
            w_done()
        norm_stage(2 * l + 1)
        if l + 1 < L:
            load_gbc(2 * (l + 1))
        else:
            load_gbc_fin()
            if t + 1 < NT:
                load_gbc(0)

        for fg in range(11):
            wg_, wg_b = w_next()
            wu_, wu_b = w_next()
            aT, aT_b = actT[fg % 2], b_actTc[fg % 2]
            for c in range(4):
                bg_, bg_b = mm_feat(wg_, c * 128, hT, all_hT, wg_b)
                bu_, bu_b = mm_feat(wu_, c * 128, hT, all_hT, wu_b)
                sg, sg_b, sg_i = tfa()
                ACT(sg[:, :], bg_[:, :], AF.Silu, [bg_b], [sg_b])
                TT(aT[:, c, :], sg[:, :], bu_[:, :], ALU.mult, [sg_b, bu_b], [aT_b[c], b_actT[fg % 2]])
                tff(sg_i)
            w_done(2)
            wd_, wd_b = w_next()
            wdv = wd_[:, :].rearrange("p (c n) -> p c n", c=4)
            groups = [(j, nb) for j in range(4) for nb in range(4)]

            def dmm(bk, bk_b, j, nb, c):
                MM(bk[:, :], aT[:, c, j * 128:(j + 1) * 128], wdv[:, c, nb * 512:(nb + 1) * 512], c == 0, c == 3,
                   [aT_b[c], wd_b], [bk_b] if c in (0, 3) else [], c == 3)

            def dadd(bk, bk_b, j, nb):
                TT(xs[:, j, nb * 512:(nb + 1) * 512], xs[:, j, nb * 512:(nb + 1) * 512], bk[:, :], ALU.add,
                   [b_x[j], bk_b], [b_x[j]])
                if fg == 10 and nb == 3:
                    norm_pre(j, l + 1 < L)

            head = []
            for (j, nb) in groups[:4]:
                bk, bk_b = bank()
                head.append((bk, bk_b, j, nb))
                for c in range(3):
                    dmm(bk, bk_b, j, nb, c)
            for (bk, bk_b, j, nb) in head:
                dmm(bk, bk_b, j, nb, 3)
                dadd(bk, bk_b, j, nb)
            for (j, nb) in groups[4:]:
                bk, bk_b = bank()
                for c in range(4):
                    dmm(bk, bk_b, j, nb, c)
                dadd(bk, bk_b, j, nb)
            w_done()
        drain()

    hT_f = hT[:, :, :].rearrange("p c t -> p (c t)").bitcast(F32)
    mixT_f = mixT[:, :, :].rearrange("p c t -> p (c t)").bitcast(F32)

    def load_x(t, j):
        r0 = t * T + j * 128
        k.dma(sp, xs[:, j, :], x_d[r0:r0 + 128, :], writes=[b_x[j]])

    load_gbc(0)
    for j in range(4):
        load_x(0, j)
        norm_pre(j, True)
    for t in range(NT):
        sl = slice(t * T, (t + 1) * T)
        for l in range(L):
            layer_tile(t, l)
        for j in range(4):
            stg, stg_b = (hT_f, all_hT) if j < 2 else (mixT_f, b_mix)
            STTe(dve, stg[:, (j % 2) * D:(j % 2 + 1) * D], xs[:, j, :], st[:, 4 + j:5 + j], gbc_fin,
                 ALU.mult, ALU.mult, [b_x[j], b_rstd[j], b_gbcf], [stg_b])
            if t + 1 < NT:
                load_x(t + 1, j)
        for half, (stg, stg_b) in enumerate(((hT_f, all_hT), (mixT_f, b_mix))):
            r0 = t * T + half * 256
            k.dma(sp, out_d[r0:r0 + 256, :].rearrange("(j p) d -> p j d", p=128),
                  stg[:, :].rearrange("p (j d) -> p j d", j=2), reads=[stg_b], writes=[b_out])
        if t + 1 < NT:
            for j in range(4):
                norm_pre(j, True)
    k.finish([b_out])
    return nc


def prep_shared(inp, L):
    f = lambda a: np.ascontiguousarray(np.asarray(a, dtype=np.float32))
    cbf, cf = make_consts()
    gv = [None] * (2 * L + 1)
    for l in range(L):
        gv[2 * l] = np.asarray(inp["attn_norm"])[l]
        gv[2 * l + 1] = np.asarray(inp["ffn_norm"])[l]
    gv[2 * L] = np.asarray(inp["final_norm"])
    gvec = f(np.stack(gv))
    lnv = []
    for l in range(L):
        lnv += [np.asarray(inp["gmlp_ln_g"])[l].reshape(512), np.asarray(inp["gmlp_ln_b"])[l].reshape(512),
                np.asarray(inp["gmlp_bs"])[l].reshape(512)]
    lnv = f(np.stack(lnv))
    ws = f(np.asarray(inp["gmlp_ws"])[:L])
    lam = f(np.concatenate([np.asarray(inp[n])[:L] for n in ("lambda_q1", "lambda_k1", "lambda_q2", "lambda_k2")], axis=1))
    cwv = np.asarray(inp["conv_w"])[:L]
    cw = f(cwv.reshape(L, 3, 4, 128).transpose(3, 0, 2, 1).reshape(128, L * 12))
    mnv = np.asarray(inp["mix_norm"])[:L]
    mn = f(mnv.reshape(L, 16, 128).transpose(2, 0, 1).reshape(128, L * 16))

    def colblocks(w, nb):
        w = np.asarray(w)[:L]
        return f(w.reshape(L, 16, 128, nb, 512).transpose(0, 3, 2, 1, 4).reshape(L, nb, 128, 8192))

    win = colblocks(inp["w_in"], 11)
    wout = colblocks(inp["w_out"], 4)
    wg = colblocks(inp["w_gate"], 11)
    wu = colblocks(inp["w_up"], 11)
    wdn = np.asarray(inp["w_down"])[:L]
    wd = f(wdn.reshape(L, 11, 4, 128, 2048).transpose(0, 1, 3, 2, 4).reshape(L, 11, 128, 8192))
    return dict(cbf=cbf, cf=cf, gvec=gvec, lnv=lnv, ws=ws, lam=lam, cw=cw, mn=mn, win=win, wout=wout, wg=wg, wu=wu, wd=wd)


def run(inp, S, L, ncores, trace=False, stop=99):
    shared = prep_shared(inp, L)
    x = np.asarray(inp["x"], dtype=np.float32)
    pos = np.asarray(inp["positions"]).astype(np.int32)
    nc = build(S, L, stop)
    in_maps = []
    for b in range(ncores):
        m = dict(shared)
        m["x"] = np.ascontiguousarray(x[b, :S])
        m["pos"] = np.ascontiguousarray(pos[b:b + 1, :S])
        in_maps.append(m)
    res = run_bass_kernel_spmd(nc, in_maps, core_ids=list(range(ncores)), trace=trace)
    out = np.stack([np.asarray(r["out"], dtype=np.float32) for r in res.results])
    return out, res


def kernel(**inputs):
    out, _ = run(inputs, 4096, 2, 8)
    return out
```

```python
import math
import contextlib
import numpy as np
import ml_dtypes
import concourse.bass as bass
import concourse.mybir as mybir
from concourse.bass_utils import run_bass_kernel_spmd

F32 = mybir.dt.float32
BF16 = mybir.dt.bfloat16
I32 = mybir.dt.int32
AF = mybir.ActivationFunctionType
ALU = mybir.AluOpType
AX = mybir.AxisListType

D = 2048
NIN = 5632
DFF = 5632
T = 512
HD = 128
RMS_EPS = 1e-6
LN_EPS = 1e-5
NSLOT = 4
NKP = 3
NTF = 8
NTB = 5
TWO_PI = 2.0 * math.pi
CW1 = 6.28125
CW2 = TWO_PI - CW1
MAGIC = 12582912.0

SAME_ENGINE_SYNC = True
FAST_RECIP = False
ACT_RSQRT = True
POOL_OFFLOAD = True
ESUM = False


class SemObj:
    def __init__(self, h):
        self.h = h
        self.count = 0
        self.owner = None
        self.maxwait = 0


class Eng:
    def __init__(self, name, h, sem, fifo=False):
        self.name = name
        self.h = h
        self.sem = sem
        sem.owner = self
        self.known = {}
        self.fifo = fifo
        self.n = 0


class Buf:
    __slots__ = ("name", "w", "r", "dsem")

    def __init__(self, name):
        self.name = name
        self.w = None
        self.r = {}
        self.dsem = None


def _flat(xs):
    out = []
    for x in xs:
        if isinstance(x, (list, tuple)):
            out.extend(_flat(x))
        elif x is not None:
            out.append(x)
    return out


class K:
    def __init__(self):
        self.nc = bass.Bass("TRN2", target_bir_lowering=False)
        self.es = contextlib.ExitStack()
        self._sems = []
        nc = self.nc
        self.pe = Eng("pe", nc.tensor, self.newsem("pe"), fifo=True)
        self.act = Eng("act", nc.scalar, self.newsem("act"))
        self.dve = Eng("dve", nc.vector, self.newsem("dve"))
        self.pool = Eng("pool", nc.gpsimd, self.newsem("pool"))
        self.sp = Eng("sp", nc.sync, self.newsem("sp"))

    def newsem(self, name):
        s = SemObj(self.es.enter_context(self.nc.semaphore(name)))
        self._sems.append(s)
        return s

    def sbuf(self, name, shape, dt):
        return self.es.enter_context(self.nc.sbuf_tensor("s_" + name, shape, dt))

    def psum(self, name, shape, dt):
        return self.es.enter_context(self.nc.psum_tensor(name, shape, dt))

    def _wait(self, eng, deps):
        best = {}
        for (s, v) in deps:
            if best.get(s, 0) < v:
                best[s] = v
        for s, v in best.items():
            if s.owner is eng:
                if eng.fifo or not SAME_ENGINE_SYNC:
                    continue
                if v > s.count:
                    continue
            if eng.known.get(s, 0) >= v:
                continue
            eng.h.wait_ge(s.h, v)
            eng.known[s] = v
            if v > s.maxwait:
                s.maxwait = v

    def _deps(self, reads, writes):
        deps = []
        for b in reads:
            if b.w is not None:
                deps.append(b.w)
        for b in writes:
            if b.w is not None:
                deps.append(b.w)
            deps.extend(b.r.items())
        return deps

    def op(self, eng, fn, reads=(), writes=(), signal=True):
        reads = _flat(reads)
        writes = _flat(writes)
        self._wait(eng, self._deps(reads, writes))
        ins = fn()
        eng.n += 1
        if signal:
            ins.then_inc(eng.sem.h, 1)
            eng.sem.count += 1
            val = eng.sem.count
        else:
            val = eng.sem.count + 1
        rec = (eng.sem, val)
        for b in reads:
            if b.r.get(rec[0], 0) < val:
                b.r[rec[0]] = val
        for b in writes:
            b.w = rec
            b.r = {}
        return ins

    def dma(self, eng, out, in_, reads=(), writes=(), sem_buf=None, **kw):
        reads = _flat(reads)
        writes = _flat(writes)
        self._wait(eng, self._deps(reads, writes))
        sb = sem_buf if sem_buf is not None else writes[0]
        if sb.dsem is None:
            sb.dsem = self.newsem("d_" + sb.name)
        ins = eng.h.dma_start(out=out, in_=in_, **kw)
        ins.then_inc(sb.dsem.h, 16)
        sb.dsem.count += 16
        rec = (sb.dsem, sb.dsem.count)
        for b in reads:
            b.r[rec[0]] = rec[1]
        for b in writes:
            b.w = rec
            b.r = {}
        return ins

    def finish(self, final_bufs):
        deps = [b.w for b in final_bufs if b.w is not None]
        for s in self._sems:
            if s.owner is None and s.count > 0:
                deps.append((s, s.count))
        self._wait(self.sp, deps)
        for s in self._sems:
            assert s.maxwait <= s.count, ("wait beyond count", s.maxwait, s.count)
        self.es.close()


NCBF = 128 * 3 + 4 * 512 + 128
C_ID, C_SW, C_ON, C_MK, C_TR = 0, 128, 256, 384, 384 + 2048


def make_consts():
    cbf = np.zeros((128, NCBF), np.float32)
    cbf[:, C_ID:C_ID + 128] = np.eye(128)
    p = np.arange(128)
    sw = np.zeros((128, 128), np.float32)
    sw[(p + 64) % 128, p] = 1.0
    cbf[:, C_SW:C_SW + 128] = sw
    cbf[:, C_ON:C_ON + 128] = 1.0
    q = np.arange(512)
    for j in range(4):
        cbf[:, C_MK + j * 512:C_MK + (j + 1) * 512] = ((j * 128 + p)[:, None] <= q[None, :])
    cbf[:, C_TR:C_TR + 128] = (p[None, :] <= p[:, None])
    cf = np.zeros((128, 4), np.float32)
    inv_freq = (1.0 / (10000.0 ** (np.arange(0, 128, 2, dtype=np.float32) / np.float32(128)))).astype(np.float32)
    cf[:, 0] = inv_freq[p % 64]
    cf[:, 1] = np.where(p < 64, -1.0, 1.0)
    return cbf.astype(ml_dtypes.bfloat16), cf


def build(S, L, stop=99):
    NT = S // T
    k = K()
    nc = k.nc
    pe, act, dve, pool, sp = k.pe, k.act, k.dve, k.pool, k.sp

    def din(name, shape, dt):
        return nc.dram_tensor(name, shape, dt, kind="ExternalInput").ap()

    x_d = din("x", [S, D], F32)
    pos_d = din("pos", [1, S], I32)
    cbf_d = din("cbf", [128, NCBF], BF16)
    cf_d = din("cf", [128, 4], F32)
    gvec_d = din("gvec", [2 * L + 1, D], F32)
    lnv_d = din("lnv", [3 * L, 512], F32)
    ws_d = din("ws", [L, 4, 128, 128], F32)
    lam_d = din("lam", [L, 512], F32)
    cw_d = din("cw", [128, L * 12], F32)
    mn_d = din("mn", [128, L * 16], F32)
    win_d = din("win", [L, 11, 128, 8192], F32)
    wout_d = din("wout", [L, 4, 128, 8192], F32)
    wg_d = din("wg", [L, 11, 128, 8192], F32)
    wu_d = din("wu", [L, 11, 128, 8192], F32)
    wd_d = din("wd", [L, 11, 128, 8192], F32)
    out_d = nc.dram_tensor("out", [S, D], F32, kind="ExternalOutput").ap()
    kc_d = nc.dram_tensor("kcache", [L, 8, 128, S], BF16, kind="Internal").ap()
    vc_d = nc.dram_tensor("vcache", [L, S, 1024], BF16, kind="Internal").ap()
    cs_d = nc.dram_tensor("cstab", [2, 128, S], F32, kind="Internal").ap()
    b_kc = [Buf(f"kc{l}") for l in range(L)]
    b_vc = [Buf(f"vc{l}") for l in range(L)]
    b_csd = Buf("csd")
    b_out = Buf("out")

    xs = k.sbuf("xs", [128, 4, D], F32)
    b_x = [Buf(f"x{j}") for j in range(4)]
    hT = k.sbuf("hT", [128, 16, T], BF16)
    b_hT = [[Buf(f"hT{j}{h}") for h in range(2)] for j in range(4)]
    mixT = k.sbuf("mixT", [128, 16, T], BF16)
    b_mix = [Buf(f"mix{c}") for c in range(16)]
    wsl = [k.sbuf(f"wsl{i}", [128, 8192], BF16) for i in range(NSLOT)]
    b_wsl = [Buf(f"wsl{i}") for i in range(NSLOT)]
    arena = k.sbuf("arena", [128, 12288], BF16)
    b_ar = [Buf(f"ar{i}") for i in range(12)]
    kp = [k.sbuf(f"kp{i}", [128, T], BF16) for i in range(NKP)]
    b_kp = [Buf(f"kp{i}") for i in range(NKP)]
    vp = [k.sbuf(f"vp{i}", [128, 4, 256], BF16) for i in range(NKP)]
    b_vp = [Buf(f"vp{i}") for i in range(NKP)]
    tfs = [k.sbuf(f"tf{i}", [128, T], F32) for i in range(NTF)]
    b_tf = [Buf(f"tf{i}") for i in range(NTF)]
    tbs = [k.sbuf(f"tb{i}", [128, T], BF16) for i in range(NTB)]
    b_tb = [Buf(f"tb{i}") for i in range(NTB)]
    cbf = k.sbuf("cbf", [128, NCBF], BF16)
    b_cbf = Buf("cbf")
    cf = k.sbuf("cf", [128, 4], F32)
    b_cf = Buf("cf")
    lnv = k.sbuf("lnv", [128, 3, 512], F32)
    b_lnv = Buf("lnv")
    wsT = k.sbuf("wsT", [128, L, 4, 128], BF16)
    b_wsT = Buf("wsT")
    cw = k.sbuf("cw", [128, L * 12], F32)
    b_cw = Buf("cw")
    mn = k.sbuf("mn", [128, L * 16], F32)
    b_mn = Buf("mn")
    nlam = k.sbuf("nlam", [128, L], F32)
    b_nlam = Buf("nlam")
    st = k.sbuf("st", [128, 64], F32)
    b_ssq = [Buf(f"ssq{j}") for j in range(4)]
    b_rstd = [Buf(f"rstd{j}") for j in range(4)]
    b_bn = [Buf(f"bn{j}") for j in range(4)]
    carry = k.sbuf("carry", [128, L, 4, 2], F32)
    b_carry = [[Buf(f"carry{l}{g}") for g in range(4)] for l in range(L)]
    xh = [k.sbuf(f"xh{i}", [128, T + 2], F32) for i in range(2)]
    b_xh = [Buf(f"xh{i}") for i in range(2)]

    o1a = k.sbuf("o1a", [128, T], F32)
    o1b = k.sbuf("o1b", [128, T], F32)
    b_o1 = [Buf("o1a"), Buf("o1b")]
    cs_tile = k.sbuf("cs_tile", [128, T], F32)
    b_cs = Buf("cs_tile")
    sn_tile = k.sbuf("sn_tile", [128, T], F32)
    b_sn = Buf("sn_tile")
    banks = [k.psum(f"bank{i}", [128, T], F32) for i in range(8)]
    b_bank = [Buf(f"bank{i}") for i in range(8)]

    def ar_bf(c0, n):
        return arena[:, c0:c0 + n]

    def ar_f32(c0, n):
        return arena[:, c0:c0 + n].bitcast(F32)

    gbc = ar_f32(0, 4096)
    b_gbc = b_ar[0:4]
    hnb = [ar_bf(4096, 2048), ar_bf(6144, 2048)]
    b_hn = [b_ar[4:6], b_ar[6:8]]
    actT = [ar_bf(8192, 2048).rearrange("p (c t) -> p c t", c=4), ar_bf(10240, 2048).rearrange("p (c t) -> p c t", c=4)]
    b_actT = [b_ar[8:10], b_ar[10:12]]
    b_actTc = [[Buf(f"actT{i}{c}") for c in range(4)] for i in range(2)]
    uT = ar_bf(0, 2048).rearrange("p (g t) -> p g t", g=4)
    b_uT = [b_ar[g // 2] for g in range(4)]
    bgT = ar_bf(2048, 2048).rearrange("p (g t) -> p g t", g=4)
    b_bg = [b_ar[2 + g // 2] for g in range(4)]
    vn = ar_bf(4096, 2048).rearrange("p (j c) -> p j c", j=4)
    b_vn = b_ar[4:6]
    cgT = ar_bf(6144, 2048).rearrange("p (g t) -> p g t", g=4)
    b_cg = [b_ar[6 + g // 2] for g in range(4)]
    qT = ar_bf(0, 4096).rearrange("p (c t) -> p c t", c=8)
    b_qT = [b_ar[c // 2] for c in range(8)]
    kTc = ar_bf(4096, 4096).rearrange("p (c t) -> p c t", c=8)
    b_kT = [b_ar[4 + c // 2] for c in range(8)]
    Vc = ar_bf(8192, 4096).rearrange("p (j c) -> p j c", j=4)
    b_V = [b_ar[8 + j] for j in range(4)]

    ident = cbf[:, C_ID:C_ID + 128]
    swap = cbf[:, C_SW:C_SW + 128]
    ones = cbf[:, C_ON:C_ON + 128]
    tril = cbf[:, C_TR:C_TR + 128]

    def mask(j):
        return cbf[:, C_MK + j * 512:C_MK + (j + 1) * 512]

    state = {"tf": 0, "tb": 0, "bank": 0, "bankset": list(range(8)), "kp": 0}

    def tf():
        i = state["tf"]
        state["tf"] = (i + 1) % NTF
        return tfs[i], b_tf[i]

    def tb():
        i = state["tb"]
        state["tb"] = (i + 1) % NTB
        return tbs[i], b_tb[i]

    def bank():
        bs = state["bankset"]
        i = bs[state["bank"] % len(bs)]
        state["bank"] += 1
        return banks[i], b_bank[i]

    def ACT(out, in_, func, reads, writes, **kw):
        return k.op(act, lambda: nc.scalar.activation(out=out, in_=in_, func=func, **kw), reads, writes)

    def TT(out, in0, in1, op, reads, writes, eng=None):
        e = eng or dve
        return k.op(e, lambda: e.h.tensor_tensor(out=out, in0=in0, in1=in1, op=op), reads, writes)

    def TS(out, in0, s1, s2, op0, op1, reads, writes):
        if op1 is None:
            return k.op(dve, lambda: nc.vector.tensor_scalar(out=out, in0=in0, scalar1=s1, scalar2=None, op0=op0), reads, writes)
        return k.op(dve, lambda: nc.vector.tensor_scalar(out=out, in0=in0, scalar1=s1, scalar2=s2, op0=op0, op1=op1), reads, writes)

    def STT(out, in0, scalar, in1, op0, op1, reads, writes):
        return k.op(dve, lambda: nc.vector.scalar_tensor_tensor(out=out, in0=in0, scalar=scalar, in1=in1, op0=op0, op1=op1), reads, writes)

    def CP(eng, out, in_, reads, writes):
        if eng is act:
            return ACT(out, in_, AF.Copy, reads, writes)
        return k.op(eng, lambda: eng.h.tensor_copy(out=out, in_=in_), reads, writes)

    def RECIP(out, in_, reads, writes):
        return k.op(dve, lambda: nc.vector.reciprocal(out=out, in_=in_), reads, writes)

    def MM(out, lhsT, rhs, start, stop, reads, writes, signal):
        return k.op(pe, lambda: nc.tensor.matmul(out, lhsT, rhs, start=start, stop=stop), reads, writes, signal=signal)

    wlist = []
    for t in range(NT):
        for l in range(L):
            for b in (0, 1, 8, 9, 10, 2, 3, 4, 5, 6, 7):
                wlist.append(win_d[l, b])
            for b in range(4):
                wlist.append(wout_d[l, b])
            for fg in range(11):
                wlist.append(wg_d[l, fg])
                wlist.append(wu_d[l, fg])
                wlist.append(wd_d[l, fg])
    wstate = {"issued": 0, "next": 0, "done": 0}

    def w_prefetch():
        while wstate["issued"] < min(wstate["done"] + NSLOT, len(wlist)):
            n = wstate["issued"]
            s = n % NSLOT
            k.dma(pool, wsl[s][:, :], wlist[n], writes=[b_wsl[s]], max_dma_last_dim=8192)
            wstate["issued"] += 1

    def w_next():
        n = wstate["next"]
        wstate["next"] += 1
        assert n < wstate["issued"], "weight block not issued"
        s = n % NSLOT
        return wsl[s], b_wsl[s]

    def w_done(cnt=1):
        wstate["done"] += cnt
        w_prefetch()

    k.dma(sp, cbf[:, :], cbf_d[:, :], writes=[b_cbf])
    k.dma(sp, cf[:, :], cf_d[:, :], writes=[b_cf])
    k.dma(sp, cw[:, :], cw_d[:, :], writes=[b_cw])
    k.dma(sp, mn[:, :], mn_d[:, :], writes=[b_mn])
    w_prefetch()
    k.op(dve, lambda: nc.vector.memset(carry[:, :, :, :], 0.0), [], [b_carry])

    for t in range(NT):
        sl = slice(t * T, (t + 1) * T)
        pi_t, pi_b = tf()
        pos_i = pi_t[:, :].bitcast(I32)
        k.dma(sp, pos_i, pos_d[0:1, sl].partition_broadcast(128), writes=[pi_b])
        pf_t, pf_b = tf()
        CP(dve, pf_t[:, :], pos_i, [pi_b], [pf_b])
        ang_t, ang_b = cs_tile, b_cs
        TS(ang_t[:, :], pf_t[:, :], cf[:, 0:1], None, ALU.mult, None, [pf_b, b_cf], [ang_b])
        for which in (1, 0):
            y_t, y_b = tf()
            if which == 0:
                TS(y_t[:, :], ang_t[:, :], math.pi / 2.0, None, ALU.add, None, [ang_b], [y_b])
                src, src_b = y_t, y_b
            else:
                src, src_b = ang_t, ang_b
            kf_t, kf_b = tf()
            TS(kf_t[:, :], src[:, :], 1.0 / TWO_PI, MAGIC, ALU.mult, ALU.add, [src_b], [kf_b])
            TS(kf_t[:, :], kf_t[:, :], -MAGIC, None, ALU.add, None, [kf_b], [kf_b])
            r_t, r_b = tf()
            STT(r_t[:, :], kf_t[:, :], -CW1, src[:, :], ALU.mult, ALU.add, [kf_b, src_b], [r_b])
            STT(r_t[:, :], kf_t[:, :], -CW2, r_t[:, :], ALU.mult, ALU.add, [kf_b, r_b], [r_b])
            TS(r_t[:, :], r_t[:, :], -math.pi, math.pi, ALU.max, ALU.min, [r_b], [r_b])
            ACT(r_t[:, :], r_t[:, :], AF.Sin, [r_b], [r_b])
            if which == 1:
                TS(r_t[:, :], r_t[:, :], cf[:, 1:2], None, ALU.mult, None, [r_b, b_cf], [r_b])
            k.dma(sp, cs_d[which, :, sl], r_t[:, :], reads=[r_b], writes=[b_csd])

    for l in range(L if stop > 0 else 0):
        lv_t, lv_b = tf()
        k.dma(sp, lv_t[:, :], lam_d[l:l + 1, :].partition_broadcast(128), writes=[lv_b])
        pr_t, pr_b = tf()
        TT(pr_t[:, 0:128], lv_t[:, 0:128], lv_t[:, 128:256], ALU.mult, [lv_b], [pr_b])
        TT(pr_t[:, 128:256], lv_t[:, 256:384], lv_t[:, 384:512], ALU.mult, [lv_b], [pr_b])
        k.op(dve, lambda: nc.vector.reduce_sum(out=st[:, 32:33], in_=pr_t[:, 0:128], axis=AX.X), [pr_b], [b_bn[0]])
        k.op(dve, lambda: nc.vector.reduce_sum(out=st[:, 33:34], in_=pr_t[:, 128:256], axis=AX.X), [pr_b], [b_bn[0]])
        ACT(st[:, 34:36], st[:, 32:34], AF.Exp, [b_bn[0]], [b_bn[1]])
        TT(st[:, 36:37], st[:, 35:36], st[:, 34:35], ALU.subtract, [b_bn[1]], [b_bn[2]])
        lam_init = 0.8 - 0.6 * math.exp(-0.3 * l)
        TS(nlam[:, l:l + 1], st[:, 36:37], -lam_init, None, ALU.add, None, [b_bn[2]], [b_nlam])
        TS(mn[:, l * 16 + 4:l * 16 + 12], mn[:, l * 16 + 4:l * 16 + 12], 1.0 - lam_init, None, ALU.mult, None, [b_mn], [b_mn])
        for g in range(4):
            w_t, w_b = tf()
            k.dma(sp, w_t[:, 0:128], ws_d[l, g], writes=[w_b])
            wm_t, wm_b = tb()
            TT(wm_t[:, 0:128], w_t[:, 0:128], tril, ALU.mult, [w_b, b_cbf], [wm_b])
            bk, bk_b = bank()
            pT = bk[:, :].bitcast(BF16)
            k.op(pe, lambda: nc.tensor.transpose(pT[:, 0:128], wm_t[:, 0:128], ident), [wm_b, b_cbf], [bk_b])
            CP(dve, wsT[:, l, g, :], pT[:, 0:128], [bk_b], [b_wsT])

    def STTe(eng, out, in0, scalar, in1, op0, op1, reads, writes):
        return k.op(eng, lambda: eng.h.scalar_tensor_tensor(out=out, in0=in0, scalar=scalar, in1=in1, op0=op0, op1=op1), reads, writes)

    junk = ar_bf(10240, 2048)
    b_junk = b_ar[10:12]

    def norm_pre(j, transposing=True):
        ACT(junk, xs[:, j, :], AF.Square, [b_x[j]], [b_junk, b_ssq[j]], accum_out=st[:, j:j + 1])
        ACT(st[:, 4 + j:5 + j], st[:, j:j + 1], AF.Ln, [b_ssq[j]], [b_rstd[j]], scale=1.0 / D, bias=RMS_EPS)
        ACT(st[:, 4 + j:5 + j], st[:, 4 + j:5 + j], AF.Exp, [b_rstd[j]], [b_rstd[j]], scale=-0.5)
        if transposing and j < 2:
            STTe(dve, hnb[j], xs[:, j, :], st[:, 4 + j:5 + j], gbc, ALU.mult, ALU.mult,
                 [b_x[j], b_rstd[j], b_gbc], [b_hn[j]])

    gbc_fin = ar_f32(4096, 4096)
    b_gbcf = b_ar[4:8]

    def load_gbc(gi):
        k.dma(sp, gbc, gvec_d[gi:gi + 1, :].partition_broadcast(128), writes=[b_gbc])

    def load_gbc_fin():
        k.dma(sp, gbc_fin, gvec_d[2 * L:2 * L + 1, :].partition_broadcast(128), writes=[b_gbcf])

    def norm_stage(gi):
        for j in range(4):
            hn, hn_b = hnb[j % 2], b_hn[j % 2]
            if j >= 2:
                STTe(dve, hn, xs[:, j, :], st[:, 4 + j:5 + j], gbc, ALU.mult, ALU.mult,
                     [b_x[j], b_rstd[j], b_gbc], [hn_b])
            for half in range(2):
                bk, bk_b = bank()
                pT = bk[:, :].bitcast(BF16).rearrange("p (c t) -> p c t", c=8)
                for c in range(8):
                    cc = half * 8 + c
                    k.op(pe, lambda: nc.tensor.transpose(pT[:, c, :], hn[:, cc * 128:(cc + 1) * 128], ident),
                         [hn_b, b_cbf], [bk_b] if c in (0, 7) else [], signal=(c == 7))
                eng = act if half == 0 else dve
                CP(eng, hT[:, half * 8:(half + 1) * 8, j * 128:(j + 1) * 128], pT[:, :, :], [bk_b], [b_hT[j][half]])

    def mm_feat(w, c0, srcT, src_bufs, w_b):
        bk, bk_b = bank()
        wv = w[:, :].rearrange("p (k c) -> p k c", k=16)
        for kc in range(16):
            MM(bk[:, :], wv[:, kc, c0:c0 + 128], srcT[:, kc, :], kc == 0, kc == 15,
               [w_b, src_bufs], [bk_b] if kc in (0, 15) else [], kc == 15)
        return bk, bk_b

    def mm_tok(w, c0, ncols, srcT, src_bufs, j, w_b, order=None):
        bk, bk_b = bank()
        wv = w[:, :].rearrange("p (k c) -> p k c", k=16)
        order = order or list(range(16))
        for i, kc in enumerate(order):
            sb_ = src_bufs[kc] if (order is not None and len(src_bufs) == 16) else src_bufs
            MM(bk[:, 0:ncols], srcT[:, kc, j * 128:(j + 1) * 128], wv[:, kc, c0:c0 + ncols], i == 0, i == 15,
               [w_b, sb_], [bk_b] if i in (0, 15) else [], i == 15)
        return bk, bk_b

    def tail(ys, y_bufs, chunk_ids, l):
        n = 128 * len(ys)
        bk, bk_b = bank()
        for i, (y, yb) in enumerate(zip(ys, y_bufs)):
            sq, sq_b = tb()
            ACT(sq[:, :], y, AF.Square, [yb], [sq_b])
            MM(bk[:, :], ones, sq[:, :], i == 0, i == len(ys) - 1, [sq_b, b_cbf], [bk_b], True)
        rt, rt_b = tf()
        ACT(rt[:, :], bk[:, :], AF.Sqrt, [bk_b], [rt_b], scale=1.0 / n, bias=RMS_EPS)
        RECIP(rt[:, :], rt[:, :], [rt_b], [rt_b])
        for y, yb, c in zip(ys, y_bufs, chunk_ids):
            STT(mixT[:, c, :], y, mn[:, l * 16 + c:l * 16 + c + 1], rt[:, :], ALU.mult, ALU.mult,
                [yb, rt_b, b_mn], [b_mix[c]])

    all_hT = [b_hT[j][h] for j in range(4) for h in range(2)]

    free_tf = list(range(NTF))
    free_tb = list(range(NTB))

    def tfa():
        assert free_tf, "out of fp32 temps"
        i = free_tf.pop(0)
        return tfs[i], b_tf[i], i

    def tff(i):
        free_tf.append(i)

    def tba():
        assert free_tb, "out of bf16 temps"
        i = free_tb.pop(0)
        return tbs[i], b_tb[i], i

    def tbf(i):
        free_tb.append(i)

    chains = []

    def spawn(gen):
        try:
            w = next(gen)
            chains.append([gen, w])
        except StopIteration:
            pass

    def tick():
        for ch in list(chains):
            ch[1] -= 1
            if ch[1] <= 0:
                try:
                    ch[1] = next(ch[0])
                except StopIteration:
                    chains.remove(ch)

    def drain():
        while chains:
            tick()

    def RECIPF(out, in_, reads, writes):
        if FAST_RECIP:
            return k.op(dve, lambda: nc.vector.reciprocal_approx_fast(out=out, in_=in_), reads, writes)
        return RECIP(out, in_, reads, writes)

    def tail_chain(ys, y_bufs, y_ids, chunk_ids, l, wait, pre=0, sq_on_dve=False):
        n = 128 * len(ys)
        if pre:
            yield pre
        sqs = []
        for y, yb in zip(ys, y_bufs):
            sq, sq_b, sq_i = tba()
            if sq_on_dve:
                TT(sq[:, :], y, y, ALU.mult, [yb], [sq_b])
            else:
                ACT(sq[:, :], y, AF.Square, [yb], [sq_b])
            sqs.append((sq, sq_b, sq_i))
        yield wait
        bk, bk_b = bank()
        for i, (sq, sq_b, sq_i) in enumerate(sqs):
            MM(bk[:, :], ones, sq[:, :], i == 0, i == len(sqs) - 1, [sq_b, b_cbf], [bk_b], True)
            tbf(sq_i)
        rt, rt_b, rt_i = tfa()
        if ACT_RSQRT:
            ACT(rt[:, :], bk[:, :], AF.Ln, [bk_b], [rt_b], scale=1.0 / n, bias=RMS_EPS)
            ACT(rt[:, :], rt[:, :], AF.Exp, [rt_b], [rt_b], scale=-0.5)
        else:
            ACT(rt[:, :], bk[:, :], AF.Sqrt, [bk_b], [rt_b], scale=1.0 / n, bias=RMS_EPS)
            RECIPF(rt[:, :], rt[:, :], [rt_b], [rt_b])
        for y, yb, yi, c in zip(ys, y_bufs, y_ids, chunk_ids):
            STT(mixT[:, c, :], y, mn[:, l * 16 + c:l * 16 + c + 1], rt[:, :], ALU.mult, ALU.mult,
                [yb, rt_b, b_mn], [b_mix[c]])
            if yi is not None:
                tff(yi)
        tff(rt_i)

    def mmf(w, c0, srcT, src_bufs, w_b):
        tick()
        return mm_feat(w, c0, srcT, src_bufs, w_b)

    def mmt(w, c0, ncols, srcT, src_bufs, j, w_b):
        tick()
        return mm_tok(w, c0, ncols, srcT, src_bufs, j, w_b)

    def layer_tile(t, l):
        sl = slice(t * T, (t + 1) * T)
        state["bankset"] = list(range(8))
        norm_stage(2 * l)
        k.dma(sp, lnv[:, :, :], lnv_d[3 * l:3 * l + 3, :].partition_broadcast(128), writes=[b_lnv])
        k.dma(sp, cs_tile[:, :], cs_d[0, :, sl], reads=[b_csd], writes=[b_cs])
        k.dma(sp, sn_tile[:, :], cs_d[1, :, sl], reads=[b_csd], writes=[b_sn])
        w, w_b = w_next()
        for g in range(4):
            bk, bk_b = mmf(w, g * 128, hT, all_hT, w_b)
            ACT(uT[:, g, :], bk[:, :], AF.Gelu, [bk_b], [b_uT[g]])
        w_done()
        w, w_b = w_next()
        vgs = []
        for j in range(4):
            bk, bk_b = mmt(w, 0, 512, hT, all_hT, j, w_b)
            vg, vg_b, vg_i = tfa()
            ACT(vg[:, :], bk[:, :], AF.Gelu, [bk_b], [vg_b])
            vgs.append((vg, vg_b, vg_i))
        w_done()
        sc, sc_b, sc_i = tfa()
        for j in range(4):
            vg, vg_b, vg_i = vgs[j]
            for g in range(4):
                o = (j * 4 + g) * 6
                k.op(dve, lambda: nc.vector.bn_stats(out=sc[:, o:o + 6], in_=vg[:, g * 128:(g + 1) * 128]), [vg_b], [sc_b])
            for g in range(4):
                o = (j * 4 + g) * 6
                m = 96 + (j * 4 + g) * 2
                k.op(dve, lambda: nc.vector.bn_aggr(out=sc[:, m:m + 2], in_=sc[:, o:o + 6]), [sc_b], [sc_b])
        var16 = sc[:, 96:128].rearrange("p (g two) -> p g two", two=2)[:, :, 1]
        ACT(sc[:, 128:144], var16, AF.Ln, [sc_b], [sc_b], bias=LN_EPS)
        ACT(sc[:, 128:144], sc[:, 128:144], AF.Exp, [sc_b], [sc_b], scale=-0.5)
        for j in range(4):
            vg, vg_b, vg_i = vgs[j]
            for g in range(4):
                m = 96 + (j * 4 + g) * 2
                r_ = 128 + j * 4 + g
                TS(vg[:, g * 128:(g + 1) * 128], vg[:, g * 128:(g + 1) * 128], sc[:, m:m + 1], sc[:, r_:r_ + 1],
                   ALU.subtract, ALU.mult, [vg_b, sc_b], [vg_b])
            TT(vg[:, :], vg[:, :], lnv[:, 0, :], ALU.mult, [vg_b, b_lnv], [vg_b])
            TT(vn[:, j, :], vg[:, :], lnv[:, 1, :], ALU.add, [vg_b, b_lnv], [b_vn])
            tff(vg_i)
        tff(sc_i)

        def chainA():
            yield 6
            for g in range(4):
                bk, bk_b = bank()
                for j in range(4):
                    MM(bk[:, j * 128:(j + 1) * 128], vn[:, j, g * 128:(g + 1) * 128], wsT[:, l, g, :], True, True,
                       [b_vn, b_wsT], [bk_b], j == 3)
                ya, ya_b, ya_i = tfa()
                for j in range(4):
                    TT(ya[:, j * 128:(j + 1) * 128], bk[:, j * 128:(j + 1) * 128], lnv[:, 2, g * 128:(g + 1) * 128], ALU.add,
                       [bk_b, b_lnv], [ya_b])
                TT(ya[:, :], ya[:, :], uT[:, g, :], ALU.mult, [ya_b, b_uT[g]], [ya_b], eng=(pool if POOL_OFFLOAD else dve))
                spawn(tail_chain([ya[:, :]], [ya_b], [ya_i], [g], l, 2))
                yield 1

        spawn(chainA())
        w, w_b = w_next()
        for g in range(4):
            bk, bk_b = mmf(w, g * 128, hT, all_hT, w_b)
            CP(act, bgT[:, g, :], bk[:, :], [bk_b], [b_bg[g]])
        w_done()
        w, w_b = w_next()
        for g in range(4):
            bk, bk_b = mmf(w, g * 128, hT, all_hT, w_b)
            CP(act, cgT[:, g, :], bk[:, :], [bk_b], [b_cg[g]])
        w_done()
        w, w_b = w_next()
        for g in range(4):
            bk, bk_b = mmf(w, g * 128, hT, all_hT, w_b)
            xe, xe_b = xh[g % 2], b_xh[g % 2]
            CP(dve, xe[:, 0:2], carry[:, l, g, :], [b_carry[l][g]], [xe_b])
            TT(xe[:, 2:T + 2], bk[:, :], cgT[:, g, :], ALU.mult, [bk_b, b_cg[g]], [xe_b])
            CP(dve, carry[:, l, g, :], xe[:, T:T + 2], [xe_b], [b_carry[l][g]])
            y, y_b, y_i = tfa()
            cwb = l * 12 + g * 3
            ce = pool if POOL_OFFLOAD else dve
            TS(y[:, :], xe[:, 2:T + 2], cw[:, cwb + 2:cwb + 3], None, ALU.mult, None, [xe_b, b_cw], [y_b])
            STT(y[:, :], xe[:, 1:T + 1], cw[:, cwb + 1:cwb + 2], y[:, :], ALU.mult, ALU.add, [xe_b, b_cw, y_b], [y_b])
            STT(y[:, :], xe[:, 0:T], cw[:, cwb:cwb + 1], y[:, :], ALU.mult, ALU.add, [xe_b, b_cw, y_b], [y_b])
            TT(y[:, :], y[:, :], bgT[:, g, :], ALU.mult, [y_b, b_bg[g]], [y_b], eng=ce)
            spawn(tail_chain([y[:, :]], [y_b], [y_i], [12 + g], l, 3))
        w_done()
        cs_t, cs_b, sn_t, sn_b = cs_tile, b_cs, sn_tile, b_sn

        def rope_chain(bk, bk_b, dst, dst_b):
            raw, raw_b, raw_i = tba()
            CP(act, raw[:, :], bk[:, :], [bk_b], [raw_b])
            t1, t1_b, t1_i = tfa()
            TT(t1[:, :], bk[:, :], cs_t[:, :], ALU.mult, [bk_b, cs_b, raw_b], [t1_b])
            yield 2
            bk2, bk2_b = bank()
            MM(bk2[:, :], swap, raw[:, :], True, True, [raw_b, b_cbf], [bk2_b], True)
            tbf(raw_i)
            t2, t2_b, t2_i = tfa()
            TT(t2[:, :], bk2[:, :], sn_t[:, :], ALU.mult, [bk2_b, sn_b], [t2_b])
            TT(dst, t1[:, :], t2[:, :], ALU.add, [t1_b, t2_b], [dst_b], eng=(pool if POOL_OFFLOAD else dve))
            tff(t1_i)
            tff(t2_i)

        for blk in range(4):
            w, w_b = w_next()
            for cc in range(4):
                c = (blk % 2) * 4 + cc
                bk, bk_b = mmf(w, cc * 128, hT, all_hT, w_b)
                if blk < 2:
                    spawn(rope_chain(bk, bk_b, qT[:, c, :], b_qT[c]))
                else:
                    spawn(rope_chain(bk, bk_b, kTc[:, c, :], b_kT[c]))
            w_done()
        for blk in range(2):
            w, w_b = w_next()
            for j in range(4):
                bk, bk_b = mmt(w, 0, 512, hT, all_hT, j, w_b)
                CP(act, Vc[:, j, blk * 512:(blk + 1) * 512], bk[:, :], [bk_b], [b_V[j]])
            w_done()
        drain()
        state["bankset"] = [0, 1]
        state["bank"] = 0
        scale = 1.0 / math.sqrt(HD)
        accsets = [[(banks[2 + e], b_bank[2 + e]) for e in range(3)], [(banks[5 + e], b_bank[5 + e]) for e in range(3)]]
        pieces = [(h, jj, p) for h in range(4) for jj in range(2) for p in range(t + 1)]
        loaded = {}
        lstate = {"n": 0}

        def load_piece(idx):
            if idx >= len(pieces) or idx in loaded:
                return
            h, jj, p = pieces[idx]
            if p == t:
                loaded[idx] = None
                return
            i = lstate["n"] % NKP
            lstate["n"] += 1
            psl = slice(p * T, (p + 1) * T)
            k.dma(sp, kp[i][:, :], kc_d[l, 2 * h + jj, :, psl], reads=[b_kc[l]], writes=[b_kp[i]])
            k.dma(sp, vp[i][:, :, :], vc_d[l, psl, h * 256:(h + 1) * 256].rearrange("(j p) c -> p j c", p=128),
                  reads=[b_vc[l]], writes=[b_vp[i]])
            loaded[idx] = i

        blocks = [(pi, b) for pi in range(len(pieces)) for b in range(4)]
        Sinfo = {}
        esum = {}

        def emit_S(bi):
            pi, b = blocks[bi]
            h, jj, p = pieces[pi]
            ch = 2 * h + jj
            if b == 0:
                load_piece(pi)
                load_piece(pi + 1)
            if b == 2:
                load_piece(pi + 2)
            if p < t:
                i = loaded[pi]
                Kb, Kb_b, c0 = kp[i][:, b * 128:(b + 1) * 128], b_kp[i], 0
            else:
                Kb, Kb_b, c0 = kTc[:, ch, b * 128:(b + 1) * 128], b_kT[ch], b * 128
            sb, sb_b = bank()
            MM(sb[:, c0:T], Kb, qT[:, ch, c0:T], True, True, [Kb_b, b_qT[ch]], [sb_b], True)
            E, E_b, E_i = tba()
            ACT(E[:, c0:T], sb[:, c0:T], AF.Exp, [sb_b], [E_b], scale=scale)
            if p == t:
                TT(E[:, c0:T], E[:, c0:T], mask(b)[:, c0:T], ALU.mult, [E_b, b_cbf], [E_b])
            if ESUM:
                if p == 0 and b == 0:
                    Es, Es_b, Es_i = tfa()
                    esum[(h, jj)] = (Es, Es_b, Es_i)
                    CP(pool, Es[:, :], E[:, :], [E_b], [Es_b])
                else:
                    Es, Es_b, Es_i = esum[(h, jj)]
                    TT(Es[:, c0:T], Es[:, c0:T], E[:, c0:T], ALU.add, [Es_b, E_b], [Es_b], eng=pool)
            Sinfo[bi] = (E, E_b, E_i, c0)

        o1 = [o1a, o1b]

        def emit_AV(bi):
            pi, b = blocks[bi]
            h, jj, p = pieces[pi]
            E, E_b, E_i, c0 = Sinfo.pop(bi)
            acc = accsets[jj]
            if p < t:
                i = loaded[pi]
                Vb, Vb_b = vp[i][:, b, :], b_vp[i]
            else:
                Vb, Vb_b = Vc[:, b, h * 256:(h + 1) * 256], b_V[b]
            first = (p == 0 and b == 0)
            last = (p == t and b == 3)
            MM(acc[0][0][:, c0:T], Vb[:, 0:128], E[:, c0:T], first, last, [Vb_b, E_b], [acc[0][1]] if (first or last) else [], False)
            MM(acc[1][0][:, c0:T], Vb[:, 128:256], E[:, c0:T], first, last, [Vb_b, E_b], [acc[1][1]] if (first or last) else [], ESUM)
            if not ESUM:
                MM(acc[2][0][:, c0:T], ones, E[:, c0:T], first, last, [E_b, b_cbf], [acc[2][1]] if (first or last) else [], True)
            tbf(E_i)
            if ESUM and last:
                Es, Es_b, Es_i = esum.pop((h, jj))
                Esb, Esb_b, Esb_i = tba()
                CP(act, Esb[:, :], Es[:, :], [Es_b], [Esb_b])
                MM(acc[2][0][:, :], ones, Esb[:, :], True, True, [Esb_b, b_cbf], [acc[2][1]], True)
                tbf(Esb_i)
                tff(Es_i)
            if last:
                r, r_b, r_i = tfa()
                if ACT_RSQRT:
                    ACT(r[:, :], acc[2][0][:, :], AF.Ln, [acc[2][1]], [r_b])
                    ACT(r[:, :], r[:, :], AF.Exp, [r_b], [r_b], scale=-1.0)
                else:
                    RECIPF(r[:, :], acc[2][0][:, :], [acc[2][1]], [r_b])
                if jj == 0:
                    for e in range(2):
                        TT(o1[e][:, :], acc[e][0][:, :], r[:, :], ALU.mult, [acc[e][1], r_b], [b_o1[e]])
                    tff(r_i)
                else:
                    for e in range(2):
                        bb, bb_b, bb_i = tfa()
                        TT(bb[:, :], acc[e][0][:, :], r[:, :], ALU.mult, [acc[e][1], r_b], [bb_b])
                        STT(o1[e][:, :], bb[:, :], nlam[:, l:l + 1], o1[e][:, :], ALU.mult, ALU.add, [bb_b, b_o1[e], b_nlam], [b_o1[e]])
                        tff(bb_i)
                    tff(r_i)
                    spawn(tail_chain([o1[0][:, :], o1[1][:, :]], [b_o1[0], b_o1[1]], [None, None], [4 + 2 * h, 5 + 2 * h], l, (2 if t == 0 else 3), pre=(1 if t == 0 else 4), sq_on_dve=True))

        nblk = len(blocks)
        emit_S(0)
        for bi in range(nblk):
            if bi + 1 < nblk:
                emit_S(bi + 1)
            emit_AV(bi)
            tick()
        state["bankset"] = list(range(8))
        load_gbc(2 * l + 1)
        if t < NT - 1:
            k.dma(sp, kc_d[l, :, :, sl].rearrange("c d t -> d c t"), kTc[:, :, :], reads=[b_kT], writes=[b_kc[l]])
            k.dma(sp, vc_d[l, sl, :].rearrange("(j p) c -> p j c", p=128), Vc[:, :, :], reads=[b_V], writes=[b_vc[l]])
        oorder = [0, 1, 2, 3, 12, 13, 14, 15, 4, 5, 6, 7, 8, 9, 10, 11]
        for nb in range(4):
            w, w_b = w_next()
            wv = w[:, :].rearrange("p (k c) -> p k c", k=16)

            def omm(bk, bk_b, j, i):
                kc = oorder[i]
                MM(bk[:, :], mixT[:, kc, j * 128:(j + 1) * 128], wv[:, kc, :], i == 0, i == 15,
                   [w_b, b_mix[kc]], [bk_b] if i in (0, 15) else [], i == 15)

            def oadd(bk, bk_b, j):
                TT(xs[:, j, nb * 512:(nb + 1) * 512], xs[:, j, nb * 512:(nb + 1) * 512], bk[:, :], ALU.add,
                   [b_x[j], bk_b], [b_x[j]])
                if nb == 3:
                    norm_pre(j, True)

            if nb == 0:
                grp = []
                for j in range(4):
                    bk, bk_b = bank()
                    grp.append((bk, bk_b))
                    for i in range(14):
                        omm(bk, bk_b, j, i)
                    tick()
                drain()
                for j in range(4):
                    bk, bk_b = grp[j]
                    omm(bk, bk_b, j, 14)
                    omm(bk, bk_b, j, 15)
                    oadd(bk, bk_b, j)
            else:
                for j in range(4):
                    bk, bk_b = bank()
                    for i in range(16):
                        omm(bk, bk_b, j, i)
                    oadd(bk, bk_b, j)
            w_done()
        norm_stage(2 * l + 1)
        if l + 1 < L:
            load_gbc(2 * (l + 1))
        else:
            load_gbc_fin()
            if t + 1 < NT:
                load_gbc(0)

        for fg in range(11):
            wg_, wg_b = w_next()
            wu_, wu_b = w_next()
            aT, aT_b = actT[fg % 2], b_actTc[fg % 2]
            for c in range(4):
                bg_, bg_b = mm_feat(wg_, c * 128, hT, all_hT, wg_b)
                bu_, bu_b = mm_feat(wu_, c * 128, hT, all_hT, wu_b)
                sg, sg_b, sg_i = tfa()
                ACT(sg[:, :], bg_[:, :], AF.Silu, [bg_b], [sg_b])
                TT(aT[:, c, :], sg[:, :], bu_[:, :], ALU.mult, [sg_b, bu_b], [aT_b[c], b_actT[fg % 2]])
                tff(sg_i)
            w_done(2)
            wd_, wd_b = w_next()
            wdv = wd_[:, :].rearrange("p (c n) -> p c n", c=4)
            groups = [(j, nb) for j in range(4) for nb in range(4)]

            def dmm(bk, bk_b, j, nb, c):
                MM(bk[:, :], aT[:, c, j * 128:(j + 1) * 128], wdv[:, c, nb * 512:(nb + 1) * 512], c == 0, c == 3,
                   [aT_b[c], wd_b], [bk_b] if c in (0, 3) else [], c == 3)

            def dadd(bk, bk_b, j, nb):
                TT(xs[:, j, nb * 512:(nb + 1) * 512], xs[:, j, nb * 512:(nb + 1) * 512], bk[:, :], ALU.add,
                   [b_x[j], bk_b], [b_x[j]])
                if fg == 10 and nb == 3:
                    norm_pre(j, l + 1 < L)

            head = []
            for (j, nb) in groups[:4]:
                bk, bk_b = bank()
                head.append((bk, bk_b, j, nb))
                for c in range(3):
                    dmm(bk, bk_b, j, nb, c)
            for (bk, bk_b, j, nb) in head:
                dmm(bk, bk_b, j, nb, 3)
                dadd(bk, bk_b, j, nb)
            for (j, nb) in groups[4:]:
                bk, bk_b = bank()
                for c in range(4):
                    dmm(bk, bk_b, j, nb, c)
                dadd(bk, bk_b, j, nb)
            w_done()
        drain()

    hT_f = hT[:, :, :].rearrange("p c t -> p (c t)").bitcast(F32)
    mixT_f = mixT[:, :, :].rearrange("p c t -> p (c t)").bitcast(F32)

    def load_x(t, j):
        r0 = t * T + j * 128
        k.dma(sp, xs[:, j, :], x_d[r0:r0 + 128, :], writes=[b_x[j]])

    load_gbc(0)
    for j in range(4):
        load_x(0, j)
        norm_pre(j, True)
    for t in range(NT):
        sl = slice(t * T, (t + 1) * T)
        for l in range(L):
            layer_tile(t, l)
        for j in range(4):
            stg, stg_b = (hT_f, all_hT) if j < 2 else (mixT_f, b_mix)
            STTe(dve, stg[:, (j % 2) * D:(j % 2 + 1) * D], xs[:, j, :], st[:, 4 + j:5 + j], gbc_fin,
                 ALU.mult, ALU.mult, [b_x[j], b_rstd[j], b_gbcf], [stg_b])
            if t + 1 < NT:
                load_x(t + 1, j)
        for half, (stg, stg_b) in enumerate(((hT_f, all_hT), (mixT_f, b_mix))):
            r0 = t * T + half * 256
            k.dma(sp, out_d[r0:r0 + 256, :].rearrange("(j p) d -> p j d", p=128),
                  stg[:, :].rearrange("p (j d) -> p j d", j=2), reads=[stg_b], writes=[b_out])
        if t + 1 < NT:
            for j in range(4):
                norm_pre(j, True)
    k.finish([b_out])
    return nc


def prep_shared(inp, L):
    f = lambda a: np.ascontiguousarray(np.asarray(a, dtype=np.float32))
    cbf, cf = make_consts()
    gv = [None] * (2 * L + 1)
    for l in range(L):
        gv[2 * l] = np.asarray(inp["attn_norm"])[l]
        gv[2 * l + 1] = np.asarray(inp["ffn_norm"])[l]
    gv[2 * L] = np.asarray(inp["final_norm"])
    gvec = f(np.stack(gv))
    lnv = []
    for l in range(L):
        lnv += [np.asarray(inp["gmlp_ln_g"])[l].reshape(512), np.asarray(inp["gmlp_ln_b"])[l].reshape(512),
                np.asarray(inp["gmlp_bs"])[l].reshape(512)]
    lnv = f(np.stack(lnv))
    ws = f(np.asarray(inp["gmlp_ws"])[:L])
    lam = f(np.concatenate([np.asarray(inp[n])[:L] for n in ("lambda_q1", "lambda_k1", "lambda_q2", "lambda_k2")], axis=1))
    cwv = np.asarray(inp["conv_w"])[:L]
    cw = f(cwv.reshape(L, 3, 4, 128).transpose(3, 0, 2, 1).reshape(128, L * 12))
    mnv = np.asarray(inp["mix_norm"])[:L]
    mn = f(mnv.reshape(L, 16, 128).transpose(2, 0, 1).reshape(128, L * 16))

    def colblocks(w, nb):
        w = np.asarray(w)[:L]
        return f(w.reshape(L, 16, 128, nb, 512).transpose(0, 3, 2, 1, 4).reshape(L, nb, 128, 8192))

    win = colblocks(inp["w_in"], 11)
    wout = colblocks(inp["w_out"], 4)
    wg = colblocks(inp["w_gate"], 11)
    wu = colblocks(inp["w_up"], 11)
    wdn = np.asarray(inp["w_down"])[:L]
    wd = f(wdn.reshape(L, 11, 4, 128, 2048).transpose(0, 1, 3, 2, 4).reshape(L, 11, 128, 8192))
    return dict(cbf=cbf, cf=cf, gvec=gvec, lnv=lnv, ws=ws, lam=lam, cw=cw, mn=mn, win=win, wout=wout, wg=wg, wu=wu, wd=wd)


def run(inp, S, L, ncores, trace=False, stop=99):
    shared = prep_shared(inp, L)
    x = np.asarray(inp["x"], dtype=np.float32)
    pos = np.asarray(inp["positions"]).astype(np.int32)
    nc = build(S, L, stop)
    in_maps = []
    for b in range(ncores):
        m = dict(shared)
        m["x"] = np.ascontiguousarray(x[b, :S])
        m["pos"] = np.ascontiguousarray(pos[b:b + 1, :S])
        in_maps.append(m)
    res = run_bass_kernel_spmd(nc, in_maps, core_ids=list(range(ncores)), trace=trace)
    out = np.stack([np.asarray(r["out"], dtype=np.float32) for r in res.results])
    return out, res


def kernel(**inputs):
    out, _ = run(inputs, 4096, 2, 8)
    return out
```

```python
import math
import contextlib
import numpy as np
import ml_dtypes
import concourse.bass as bass
import concourse.mybir as mybir
from concourse.bass_utils import run_bass_kernel_spmd

F32 = mybir.dt.float32
BF16 = mybir.dt.bfloat16
I32 = mybir.dt.int32
AF = mybir.ActivationFunctionType
ALU = mybir.AluOpType
AX = mybir.AxisListType

D = 2048
NIN = 5632
DFF = 5632
T = 512
HD = 128
RMS_EPS = 1e-6
LN_EPS = 1e-5
NSLOT = 4
NKP = 3
NTF = 8
NTB = 5
TWO_PI = 2.0 * math.pi
CW1 = 6.28125
CW2 = TWO_PI - CW1
MAGIC = 12582912.0

SAME_ENGINE_SYNC = True
FAST_RECIP = False
ACT_RSQRT = True
POOL_OFFLOAD = True
ESUM = False


class SemObj:
    def __init__(self, h):
        self.h = h
        self.count = 0
        self.owner = None
        self.maxwait = 0


class Eng:
    def __init__(self, name, h, sem, fifo=False):
        self.name = name
        self.h = h
        self.sem = sem
        sem.owner = self
        self.known = {}
        self.fifo = fifo
        self.n = 0


class Buf:
    __slots__ = ("name", "w", "r", "dsem")

    def __init__(self, name):
        self.name = name
        self.w = None
        self.r = {}
        self.dsem = None


def _flat(xs):
    out = []
    for x in xs:
        if isinstance(x, (list, tuple)):
            out.extend(_flat(x))
        elif x is not None:
            out.append(x)
    return out


class K:
    def __init__(self):
        self.nc = bass.Bass("TRN2", target_bir_lowering=False)
        self.es = contextlib.ExitStack()
        self._sems = []
        nc = self.nc
        self.pe = Eng("pe", nc.tensor, self.newsem("pe"), fifo=True)
        self.act = Eng("act", nc.scalar, self.newsem("act"))
        self.dve = Eng("dve", nc.vector, self.newsem("dve"))
        self.pool = Eng("pool", nc.gpsimd, self.newsem("pool"))
        self.sp = Eng("sp", nc.sync, self.newsem("sp"))

    def newsem(self, name):
        s = SemObj(self.es.enter_context(self.nc.semaphore(name)))
        self._sems.append(s)
        return s

    def sbuf(self, name, shape, dt):
        return self.es.enter_context(self.nc.sbuf_tensor("s_" + name, shape, dt))

    def psum(self, name, shape, dt):
        return self.es.enter_context(self.nc.psum_tensor(name, shape, dt))

    def _wait(self, eng, deps):
        best = {}
        for (s, v) in deps:
            if best.get(s, 0) < v:
                best[s] = v
        for s, v in best.items():
            if s.owner is eng:
                if eng.fifo or not SAME_ENGINE_SYNC:
                    continue
                if v > s.count:
                    continue
            if eng.known.get(s, 0) >= v:
                continue
            eng.h.wait_ge(s.h, v)
            eng.known[s] = v
            if v > s.maxwait:
                s.maxwait = v

    def _deps(self, reads, writes):
        deps = []
        for b in reads:
            if b.w is not None:
                deps.append(b.w)
        for b in writes:
            if b.w is not None:
                deps.append(b.w)
            deps.extend(b.r.items())
        return deps

    def op(self, eng, fn, reads=(), writes=(), signal=True):
        reads = _flat(reads)
        writes = _flat(writes)
        self._wait(eng, self._deps(reads, writes))
        ins = fn()
        eng.n += 1
        if signal:
            ins.then_inc(eng.sem.h, 1)
            eng.sem.count += 1
            val = eng.sem.count
        else:
            val = eng.sem.count + 1
        rec = (eng.sem, val)
        for b in reads:
            if b.r.get(rec[0], 0) < val:
                b.r[rec[0]] = val
        for b in writes:
            b.w = rec
            b.r = {}
        return ins

    def dma(self, eng, out, in_, reads=(), writes=(), sem_buf=None, **kw):
        reads = _flat(reads)
        writes = _flat(writes)
        self._wait(eng, self._deps(reads, writes))
        sb = sem_buf if sem_buf is not None else writes[0]
        if sb.dsem is None:
            sb.dsem = self.newsem("d_" + sb.name)
        ins = eng.h.dma_start(out=out, in_=in_, **kw)
        ins.then_inc(sb.dsem.h, 16)
        sb.dsem.count += 16
        rec = (sb.dsem, sb.dsem.count)
        for b in reads:
            b.r[rec[0]] = rec[1]
        for b in writes:
            b.w = rec
            b.r = {}
        return ins

    def finish(self, final_bufs):
        deps = [b.w for b in final_bufs if b.w is not None]
        for s in self._sems:
            if s.owner is None and s.count > 0:
                deps.append((s, s.count))
        self._wait(self.sp, deps)
        for s in self._sems:
            assert s.maxwait <= s.count, ("wait beyond count", s.maxwait, s.count)
        self.es.close()


NCBF = 128 * 3 + 4 * 512 + 128
C_ID, C_SW, C_ON, C_MK, C_TR = 0, 128, 256, 384, 384 + 2048


def make_consts():
    cbf = np.zeros((128, NCBF), np.float32)
    cbf[:, C_ID:C_ID + 128] = np.eye(128)
    p = np.arange(128)
    sw = np.zeros((128, 128), np.float32)
    sw[(p + 64) % 128, p] = 1.0
    cbf[:, C_SW:C_SW + 128] = sw
    cbf[:, C_ON:C_ON + 128] = 1.0
    q = np.arange(512)
    for j in range(4):
        cbf[:, C_MK + j * 512:C_MK + (j + 1) * 512] = ((j * 128 + p)[:, None] <= q[None, :])
    cbf[:, C_TR:C_TR + 128] = (p[None, :] <= p[:, None])
    cf = np.zeros((128, 4), np.float32)
    inv_freq = (1.0 / (10000.0 ** (np.arange(0, 128, 2, dtype=np.float32) / np.float32(128)))).astype(np.float32)
    cf[:, 0] = inv_freq[p % 64]
    cf[:, 1] = np.where(p < 64, -1.0, 1.0)
    return cbf.astype(ml_dtypes.bfloat16), cf


def build(S, L, stop=99):
    NT = S // T
    k = K()
    nc = k.nc
    pe, act, dve, pool, sp = k.pe, k.act, k.dve, k.pool, k.sp

    def din(name, shape, dt):
        return nc.dram_tensor(name, shape, dt, kind="ExternalInput").ap()

    x_d = din("x", [S, D], F32)
    pos_d = din("pos", [1, S], I32)
    cbf_d = din("cbf", [128, NCBF], BF16)
    cf_d = din("cf", [128, 4], F32)
    gvec_d = din("gvec", [2 * L + 1, D], F32)
    lnv_d = din("lnv", [3 * L, 512], F32)
    ws_d = din("ws", [L, 4, 128, 128], F32)
    lam_d = din("lam", [L, 512], F32)
    cw_d = din("cw", [128, L * 12], F32)
    mn_d = din("mn", [128, L * 16], F32)
    win_d = din("win", [L, 11, 128, 8192], F32)
    wout_d = din("wout", [L, 4, 128, 8192], F32)
    wg_d = din("wg", [L, 11, 128, 8192], F32)
    wu_d = din("wu", [L, 11, 128, 8192], F32)
    wd_d = din("wd", [L, 11, 128, 8192], F32)
    out_d = nc.dram_tensor("out", [S, D], F32, kind="ExternalOutput").ap()
    kc_d = nc.dram_tensor("kcache", [L, 8, 128, S], BF16, kind="Internal").ap()
    vc_d = nc.dram_tensor("vcache", [L, S, 1024], BF16, kind="Internal").ap()
    cs_d = nc.dram_tensor("cstab", [2, 128, S], F32, kind="Internal").ap()
    b_kc = [Buf(f"kc{l}") for l in range(L)]
    b_vc = [Buf(f"vc{l}") for l in range(L)]
    b_csd = Buf("csd")
    b_out = Buf("out")

    xs = k.sbuf("xs", [128, 4, D], F32)
    b_x = [Buf(f"x{j}") for j in range(4)]
    hT = k.sbuf("hT", [128, 16, T], BF16)
    b_hT = [[Buf(f"hT{j}{h}") for h in range(2)] for j in range(4)]
    mixT = k.sbuf("mixT", [128, 16, T], BF16)
    b_mix = [Buf(f"mix{c}") for c in range(16)]
    wsl = [k.sbuf(f"wsl{i}", [128, 8192], BF16) for i in range(NSLOT)]
    b_wsl = [Buf(f"wsl{i}") for i in range(NSLOT)]
    arena = k.sbuf("arena", [128, 12288], BF16)
    b_ar = [Buf(f"ar{i}") for i in range(12)]
    kp = [k.sbuf(f"kp{i}", [128, T], BF16) for i in range(NKP)]
    b_kp = [Buf(f"kp{i}") for i in range(NKP)]
    vp = [k.sbuf(f"vp{i}", [128, 4, 256], BF16) for i in range(NKP)]
    b_vp = [Buf(f"vp{i}") for i in range(NKP)]
    tfs = [k.sbuf(f"tf{i}", [128, T], F32) for i in range(NTF)]
    b_tf = [Buf(f"tf{i}") for i in range(NTF)]
    tbs = [k.sbuf(f"tb{i}", [128, T], BF16) for i in range(NTB)]
    b_tb = [Buf(f"tb{i}") for i in range(NTB)]
    cbf = k.sbuf("cbf", [128, NCBF], BF16)
    b_cbf = Buf("cbf")
    cf = k.sbuf("cf", [128, 4], F32)
    b_cf = Buf("cf")
    lnv = k.sbuf("lnv", [128, 3, 512], F32)
    b_lnv = Buf("lnv")
    wsT = k.sbuf("wsT", [128, L, 4, 128], BF16)
    b_wsT = Buf("wsT")
    cw = k.sbuf("cw", [128, L * 12], F32)
    b_cw = Buf("cw")
    mn = k.sbuf("mn", [128, L * 16], F32)
    b_mn = Buf("mn")
    nlam = k.sbuf("nlam", [128, L], F32)
    b_nlam = Buf("nlam")
    st = k.sbuf("st", [128, 64], F32)
    b_ssq = [Buf(f"ssq{j}") for j in range(4)]
    b_rstd = [Buf(f"rstd{j}") for j in range(4)]
    b_bn = [Buf(f"bn{j}") for j in range(4)]
    carry = k.sbuf("carry", [128, L, 4, 2], F32)
    b_carry = [[Buf(f"carry{l}{g}") for g in range(4)] for l in range(L)]
    xh = [k.sbuf(f"xh{i}", [128, T + 2], F32) for i in range(2)]
    b_xh = [Buf(f"xh{i}") for i in range(2)]

    o1a = k.sbuf("o1a", [128, T], F32)
    o1b = k.sbuf("o1b", [128, T], F32)
    b_o1 = [Buf("o1a"), Buf("o1b")]
    cs_tile = k.sbuf("cs_tile", [128, T], F32)
    b_cs = Buf("cs_tile")
    sn_tile = k.sbuf("sn_tile", [128, T], F32)
    b_sn = Buf("sn_tile")
    banks = [k.psum(f"bank{i}", [128, T], F32) for i in range(8)]
    b_bank = [Buf(f"bank{i}") for i in range(8)]

    def ar_bf(c0, n):
        return arena[:, c0:c0 + n]

    def ar_f32(c0, n):
        return arena[:, c0:c0 + n].bitcast(F32)

    gbc = ar_f32(0, 4096)
    b_gbc = b_ar[0:4]
    hnb = [ar_bf(4096, 2048), ar_bf(6144, 2048)]
    b_hn = [b_ar[4:6], b_ar[6:8]]
    actT = [ar_bf(8192, 2048).rearrange("p (c t) -> p c t", c=4), ar_bf(10240, 2048).rearrange("p (c t) -> p c t", c=4)]
    b_actT = [b_ar[8:10], b_ar[10:12]]
    b_actTc = [[Buf(f"actT{i}{c}") for c in range(4)] for i in range(2)]
    uT = ar_bf(0, 2048).rearrange("p (g t) -> p g t", g=4)
    b_uT = [b_ar[g // 2] for g in range(4)]
    bgT = ar_bf(2048, 2048).rearrange("p (g t) -> p g t", g=4)
    b_bg = [b_ar[2 + g // 2] for g in range(4)]
    vn = ar_bf(4096, 2048).rearrange("p (j c) -> p j c", j=4)
    b_vn = b_ar[4:6]
    cgT = ar_bf(6144, 2048).rearrange("p (g t) -> p g t", g=4)
    b_cg = [b_ar[6 + g // 2] for g in range(4)]
    qT = ar_bf(0, 4096).rearrange("p (c t) -> p c t", c=8)
    b_qT = [b_ar[c // 2] for c in range(8)]
    kTc = ar_bf(4096, 4096).rearrange("p (c t) -> p c t", c=8)
    b_kT = [b_ar[4 + c // 2] for c in range(8)]
    Vc = ar_bf(8192, 4096).rearrange("p (j c) -> p j c", j=4)
    b_V = [b_ar[8 + j] for j in range(4)]

    ident = cbf[:, C_ID:C_ID + 128]
    swap = cbf[:, C_SW:C_SW + 128]
    ones = cbf[:, C_ON:C_ON + 128]
    tril = cbf[:, C_TR:C_TR + 128]

    def mask(j):
        return cbf[:, C_MK + j * 512:C_MK + (j + 1) * 512]

    state = {"tf": 0, "tb": 0, "bank": 0, "bankset": list(range(8)), "kp": 0}

    def tf():
        i = state["tf"]
        state["tf"] = (i + 1) % NTF
        return tfs[i], b_tf[i]

    def tb():
        i = state["tb"]
        state["tb"] = (i + 1) % NTB
        return tbs[i], b_tb[i]

    def bank():
        bs = state["bankset"]
        i = bs[state["bank"] % len(bs)]
        state["bank"] += 1
        return banks[i], b_bank[i]

    def ACT(out, in_, func, reads, writes, **kw):
        return k.op(act, lambda: nc.scalar.activation(out=out, in_=in_, func=func, **kw), reads, writes)

    def TT(out, in0, in1, op, reads, writes, eng=None):
        e = eng or dve
        return k.op(e, lambda: e.h.tensor_tensor(out=out, in0=in0, in1=in1, op=op), reads, writes)

    def TS(out, in0, s1, s2, op0, op1, reads, writes):
        if op1 is None:
            return k.op(dve, lambda: nc.vector.tensor_scalar(out=out, in0=in0, scalar1=s1, scalar2=None, op0=op0), reads, writes)
        return k.op(dve, lambda: nc.vector.tensor_scalar(out=out, in0=in0, scalar1=s1, scalar2=s2, op0=op0, op1=op1), reads, writes)

    def STT(out, in0, scalar, in1, op0, op1, reads, writes):
        return k.op(dve, lambda: nc.vector.scalar_tensor_tensor(out=out, in0=in0, scalar=scalar, in1=in1, op0=op0, op1=op1), reads, writes)

    def CP(eng, out, in_, reads, writes):
        if eng is act:
            return ACT(out, in_, AF.Copy, reads, writes)
        return k.op(eng, lambda: eng.h.tensor_copy(out=out, in_=in_), reads, writes)

    def RECIP(out, in_, reads, writes):
        return k.op(dve, lambda: nc.vector.reciprocal(out=out, in_=in_), reads, writes)

    def MM(out, lhsT, rhs, start, stop, reads, writes, signal):
        return k.op(pe, lambda: nc.tensor.matmul(out, lhsT, rhs, start=start, stop=stop), reads, writes, signal=signal)

    wlist = []
    for t in range(NT):
        for l in range(L):
            for b in (0, 1, 8, 9, 10, 2, 3, 4, 5, 6, 7):
                wlist.append(win_d[l, b])
            for b in range(4):
                wlist.append(wout_d[l, b])
            for fg in range(11):
                wlist.append(wg_d[l, fg])
                wlist.append(wu_d[l, fg])
                wlist.append(wd_d[l, fg])
    wstate = {"issued": 0, "next": 0, "done": 0}

    def w_prefetch():
        while wstate["issued"] < min(wstate["done"] + NSLOT, len(wlist)):
            n = wstate["issued"]
            s = n % NSLOT
            k.dma(pool, wsl[s][:, :], wlist[n], writes=[b_wsl[s]], max_dma_last_dim=8192)
            wstate["issued"] += 1

    def w_next():
        n = wstate["next"]
        wstate["next"] += 1
        assert n < wstate["issued"], "weight block not issued"
        s = n % NSLOT
        return wsl[s], b_wsl[s]

    def w_done(cnt=1):
        wstate["done"] += cnt
        w_prefetch()

    k.dma(sp, cbf[:, :], cbf_d[:, :], writes=[b_cbf])
    k.dma(sp, cf[:, :], cf_d[:, :], writes=[b_cf])
    k.dma(sp, cw[:, :], cw_d[:, :], writes=[b_cw])
    k.dma(sp, mn[:, :], mn_d[:, :], writes=[b_mn])
    w_prefetch()
    k.op(dve, lambda: nc.vector.memset(carry[:, :, :, :], 0.0), [], [b_carry])

    for t in range(NT):
        sl = slice(t * T, (t + 1) * T)
        pi_t, pi_b = tf()
        pos_i = pi_t[:, :].bitcast(I32)
        k.dma(sp, pos_i, pos_d[0:1, sl].partition_broadcast(128), writes=[pi_b])
        pf_t, pf_b = tf()
        CP(dve, pf_t[:, :], pos_i, [pi_b], [pf_b])
        ang_t, ang_b = cs_tile, b_cs
        TS(ang_t[:, :], pf_t[:, :], cf[:, 0:1], None, ALU.mult, None, [pf_b, b_cf], [ang_b])
        for which in (1, 0):
            y_t, y_b = tf()
            if which == 0:
                TS(y_t[:, :], ang_t[:, :], math.pi / 2.0, None, ALU.add, None, [ang_b], [y_b])
                src, src_b = y_t, y_b
            else:
                src, src_b = ang_t, ang_b
            kf_t, kf_b = tf()
            TS(kf_t[:, :], src[:, :], 1.0 / TWO_PI, MAGIC, ALU.mult, ALU.add, [src_b], [kf_b])
            TS(kf_t[:, :], kf_t[:, :], -MAGIC, None, ALU.add, None, [kf_b], [kf_b])
            r_t, r_b = tf()
            STT(r_t[:, :], kf_t[:, :], -CW1, src[:, :], ALU.mult, ALU.add, [kf_b, src_b], [r_b])
            STT(r_t[:, :], kf_t[:, :], -CW2, r_t[:, :], ALU.mult, ALU.add, [kf_b, r_b], [r_b])
            TS(r_t[:, :], r_t[:, :], -math.pi, math.pi, ALU.max, ALU.min, [r_b], [r_b])
            ACT(r_t[:, :], r_t[:, :], AF.Sin, [r_b], [r_b])
            if which == 1:
                TS(r_t[:, :], r_t[:, :], cf[:, 1:2], None, ALU.mult, None, [r_b, b_cf], [r_b])
            k.dma(sp, cs_d[which, :, sl], r_t[:, :], reads=[r_b], writes=[b_csd])

    for l in range(L if stop > 0 else 0):
        lv_t, lv_b = tf()
        k.dma(sp, lv_t[:, :], lam_d[l:l + 1, :].partition_broadcast(128), writes=[lv_b])
        pr_t, pr_b = tf()
        TT(pr_t[:, 0:128], lv_t[:, 0:128], lv_t[:, 128:256], ALU.mult, [lv_b], [pr_b])
        TT(pr_t[:, 128:256], lv_t[:, 256:384], lv_t[:, 384:512], ALU.mult, [lv_b], [pr_b])
        k.op(dve, lambda: nc.vector.reduce_sum(out=st[:, 32:33], in_=pr_t[:, 0:128], axis=AX.X), [pr_b], [b_bn[0]])
        k.op(dve, lambda: nc.vector.reduce_sum(out=st[:, 33:34], in_=pr_t[:, 128:256], axis=AX.X), [pr_b], [b_bn[0]])
        ACT(st[:, 34:36], st[:, 32:34], AF.Exp, [b_bn[0]], [b_bn[1]])
        TT(st[:, 36:37], st[:, 35:36], st[:, 34:35], ALU.subtract, [b_bn[1]], [b_bn[2]])
        lam_init = 0.8 - 0.6 * math.exp(-0.3 * l)
        TS(nlam[:, l:l + 1], st[:, 36:37], -lam_init, None, ALU.add, None, [b_bn[2]], [b_nlam])
        TS(mn[:, l * 16 + 4:l * 16 + 12], mn[:, l * 16 + 4:l * 16 + 12], 1.0 - lam_init, None, ALU.mult, None, [b_mn], [b_mn])
        for g in range(4):
            w_t, w_b = tf()
            k.dma(sp, w_t[:, 0:128], ws_d[l, g], writes=[w_b])
            wm_t, wm_b = tb()
            TT(wm_t[:, 0:128], w_t[:, 0:128], tril, ALU.mult, [w_b, b_cbf], [wm_b])
            bk, bk_b = bank()
            pT = bk[:, :].bitcast(BF16)
            k.op(pe, lambda: nc.tensor.transpose(pT[:, 0:128], wm_t[:, 0:128], ident), [wm_b, b_cbf], [bk_b])
            CP(dve, wsT[:, l, g, :], pT[:, 0:128], [bk_b], [b_wsT])

    def STTe(eng, out, in0, scalar, in1, op0, op1, reads, writes):
        return k.op(eng, lambda: eng.h.scalar_tensor_tensor(out=out, in0=in0, scalar=scalar, in1=in1, op0=op0, op1=op1), reads, writes)

    junk = ar_bf(10240, 2048)
    b_junk = b_ar[10:12]

    def norm_pre(j, transposing=True):
        ACT(junk, xs[:, j, :], AF.Square, [b_x[j]], [b_junk, b_ssq[j]], accum_out=st[:, j:j + 1])
        ACT(st[:, 4 + j:5 + j], st[:, j:j + 1], AF.Ln, [b_ssq[j]], [b_rstd[j]], scale=1.0 / D, bias=RMS_EPS)
        ACT(st[:, 4 + j:5 + j], st[:, 4 + j:5 + j], AF.Exp, [b_rstd[j]], [b_rstd[j]], scale=-0.5)
        if transposing and j < 2 and not DELAY_STT:
            norm_stt(j)

    DELAY_STT = True

    def norm_stt(j):
        STTe(dve, hnb[j], xs[:, j, :], st[:, 4 + j:5 + j], gbc, ALU.mult, ALU.mult,
             [b_x[j], b_rstd[j], b_gbc], [b_hn[j]])

    gbc_fin = ar_f32(4096, 4096)
    b_gbcf = b_ar[4:8]

    def load_gbc(gi):
        k.dma(sp, gbc, gvec_d[gi:gi + 1, :].partition_broadcast(128), writes=[b_gbc])

    def load_gbc_fin():
        k.dma(sp, gbc_fin, gvec_d[2 * L:2 * L + 1, :].partition_broadcast(128), writes=[b_gbcf])

    def norm_stage(gi):
        for j in range(4):
            hn, hn_b = hnb[j % 2], b_hn[j % 2]
            if j >= 2:
                STTe(dve, hn, xs[:, j, :], st[:, 4 + j:5 + j], gbc, ALU.mult, ALU.mult,
                     [b_x[j], b_rstd[j], b_gbc], [hn_b])
            for half in range(2):
                bk, bk_b = bank()
                pT = bk[:, :].bitcast(BF16).rearrange("p (c t) -> p c t", c=8)
                for c in range(8):
                    cc = half * 8 + c
                    k.op(pe, lambda: nc.tensor.transpose(pT[:, c, :], hn[:, cc * 128:(cc + 1) * 128], ident),
                         [hn_b, b_cbf], [bk_b] if c in (0, 7) else [], signal=(c == 7))
                eng = act if half == 0 else dve
                CP(eng, hT[:, half * 8:(half + 1) * 8, j * 128:(j + 1) * 128], pT[:, :, :], [bk_b], [b_hT[j][half]])

    def mm_feat(w, c0, srcT, src_bufs, w_b):
        bk, bk_b = bank()
        wv = w[:, :].rearrange("p (k c) -> p k c", k=16)
        for kc in range(16):
            MM(bk[:, :], wv[:, kc, c0:c0 + 128], srcT[:, kc, :], kc == 0, kc == 15,
               [w_b, src_bufs], [bk_b] if kc in (0, 15) else [], kc == 15)
        return bk, bk_b

    def mm_tok(w, c0, ncols, srcT, src_bufs, j, w_b, order=None):
        bk, bk_b = bank()
        wv = w[:, :].rearrange("p (k c) -> p k c", k=16)
        order = order or list(range(16))
        for i, kc in enumerate(order):
            sb_ = src_bufs[kc] if (order is not None and len(src_bufs) == 16) else src_bufs
            MM(bk[:, 0:ncols], srcT[:, kc, j * 128:(j + 1) * 128], wv[:, kc, c0:c0 + ncols], i == 0, i == 15,
               [w_b, sb_], [bk_b] if i in (0, 15) else [], i == 15)
        return bk, bk_b

    def tail(ys, y_bufs, chunk_ids, l):
        n = 128 * len(ys)
        bk, bk_b = bank()
        for i, (y, yb) in enumerate(zip(ys, y_bufs)):
            sq, sq_b = tb()
            ACT(sq[:, :], y, AF.Square, [yb], [sq_b])
            MM(bk[:, :], ones, sq[:, :], i == 0, i == len(ys) - 1, [sq_b, b_cbf], [bk_b], True)
        rt, rt_b = tf()
        ACT(rt[:, :], bk[:, :], AF.Sqrt, [bk_b], [rt_b], scale=1.0 / n, bias=RMS_EPS)
        RECIP(rt[:, :], rt[:, :], [rt_b], [rt_b])
        for y, yb, c in zip(ys, y_bufs, chunk_ids):
            STT(mixT[:, c, :], y, mn[:, l * 16 + c:l * 16 + c + 1], rt[:, :], ALU.mult, ALU.mult,
                [yb, rt_b, b_mn], [b_mix[c]])

    all_hT = [b_hT[j][h] for j in range(4) for h in range(2)]

    free_tf = list(range(NTF))
    free_tb = list(range(NTB))

    def tfa():
        assert free_tf, "out of fp32 temps"
        i = free_tf.pop(0)
        return tfs[i], b_tf[i], i

    def tff(i):
        free_tf.append(i)

    def tba():
        assert free_tb, "out of bf16 temps"
        i = free_tb.pop(0)
        return tbs[i], b_tb[i], i

    def tbf(i):
        free_tb.append(i)

    chains = []

    def spawn(gen):
        try:
            w = next(gen)
            chains.append([gen, w])
        except StopIteration:
            pass

    def tick():
        for ch in list(chains):
            ch[1] -= 1
            if ch[1] <= 0:
                try:
                    ch[1] = next(ch[0])
                except StopIteration:
                    chains.remove(ch)

    def drain():
        while chains:
            tick()

    def RECIPF(out, in_, reads, writes):
        if FAST_RECIP:
            return k.op(dve, lambda: nc.vector.reciprocal_approx_fast(out=out, in_=in_), reads, writes)
        return RECIP(out, in_, reads, writes)

    def tail_chain(ys, y_bufs, y_ids, chunk_ids, l, wait, pre=0, sq_on_dve=False):
        n = 128 * len(ys)
        if pre:
            yield pre
        sqs = []
        for y, yb in zip(ys, y_bufs):
            sq, sq_b, sq_i = tba()
            if sq_on_dve:
                TT(sq[:, :], y, y, ALU.mult, [yb], [sq_b])
            else:
                ACT(sq[:, :], y, AF.Square, [yb], [sq_b])
            sqs.append((sq, sq_b, sq_i))
        yield wait
        bk, bk_b = bank()
        for i, (sq, sq_b, sq_i) in enumerate(sqs):
            MM(bk[:, :], ones, sq[:, :], i == 0, i == len(sqs) - 1, [sq_b, b_cbf], [bk_b], True)
            tbf(sq_i)
        rt, rt_b, rt_i = tfa()
        if ACT_RSQRT:
            ACT(rt[:, :], bk[:, :], AF.Ln, [bk_b], [rt_b], scale=1.0 / n, bias=RMS_EPS)
            ACT(rt[:, :], rt[:, :], AF.Exp, [rt_b], [rt_b], scale=-0.5)
        else:
            ACT(rt[:, :], bk[:, :], AF.Sqrt, [bk_b], [rt_b], scale=1.0 / n, bias=RMS_EPS)
            RECIPF(rt[:, :], rt[:, :], [rt_b], [rt_b])
        for y, yb, yi, c in zip(ys, y_bufs, y_ids, chunk_ids):
            STT(mixT[:, c, :], y, mn[:, l * 16 + c:l * 16 + c + 1], rt[:, :], ALU.mult, ALU.mult,
                [yb, rt_b, b_mn], [b_mix[c]])
            if yi is not None:
                tff(yi)
        tff(rt_i)

    def mmf(w, c0, srcT, src_bufs, w_b):
        tick()
        return mm_feat(w, c0, srcT, src_bufs, w_b)

    def mmt(w, c0, ncols, srcT, src_bufs, j, w_b):
        tick()
        return mm_tok(w, c0, ncols, srcT, src_bufs, j, w_b)

    def layer_tile(t, l):
        sl = slice(t * T, (t + 1) * T)
        state["bankset"] = list(range(8))
        norm_stage(2 * l)
        k.dma(sp, lnv[:, :, :], lnv_d[3 * l:3 * l + 3, :].partition_broadcast(128), writes=[b_lnv])
        k.dma(sp, cs_tile[:, :], cs_d[0, :, sl], reads=[b_csd], writes=[b_cs])
        k.dma(sp, sn_tile[:, :], cs_d[1, :, sl], reads=[b_csd], writes=[b_sn])
        w, w_b = w_next()
        for g in range(4):
            bk, bk_b = mmf(w, g * 128, hT, all_hT, w_b)
            ACT(uT[:, g, :], bk[:, :], AF.Gelu, [bk_b], [b_uT[g]])
        w_done()
        w, w_b = w_next()
        vgs = []
        for j in range(4):
            bk, bk_b = mmt(w, 0, 512, hT, all_hT, j, w_b)
            vg, vg_b, vg_i = tfa()
            ACT(vg[:, :], bk[:, :], AF.Gelu, [bk_b], [vg_b])
            vgs.append((vg, vg_b, vg_i))
        w_done()
        sc, sc_b, sc_i = tfa()
        for j in range(4):
            vg, vg_b, vg_i = vgs[j]
            for g in range(4):
                o = (j * 4 + g) * 6
                k.op(dve, lambda: nc.vector.bn_stats(out=sc[:, o:o + 6], in_=vg[:, g * 128:(g + 1) * 128]), [vg_b], [sc_b])
            for g in range(4):
                o = (j * 4 + g) * 6
                m = 96 + (j * 4 + g) * 2
                k.op(dve, lambda: nc.vector.bn_aggr(out=sc[:, m:m + 2], in_=sc[:, o:o + 6]), [sc_b], [sc_b])
        var16 = sc[:, 96:128].rearrange("p (g two) -> p g two", two=2)[:, :, 1]
        ACT(sc[:, 128:144], var16, AF.Ln, [sc_b], [sc_b], bias=LN_EPS)
        ACT(sc[:, 128:144], sc[:, 128:144], AF.Exp, [sc_b], [sc_b], scale=-0.5)
        for j in range(4):
            vg, vg_b, vg_i = vgs[j]
            for g in range(4):
                m = 96 + (j * 4 + g) * 2
                r_ = 128 + j * 4 + g
                TS(vg[:, g * 128:(g + 1) * 128], vg[:, g * 128:(g + 1) * 128], sc[:, m:m + 1], sc[:, r_:r_ + 1],
                   ALU.subtract, ALU.mult, [vg_b, sc_b], [vg_b])
            TT(vg[:, :], vg[:, :], lnv[:, 0, :], ALU.mult, [vg_b, b_lnv], [vg_b])
            TT(vn[:, j, :], vg[:, :], lnv[:, 1, :], ALU.add, [vg_b, b_lnv], [b_vn])
            tff(vg_i)
        tff(sc_i)

        def chainA():
            yield 6
            for g in range(4):
                bk, bk_b = bank()
                for j in range(4):
                    MM(bk[:, j * 128:(j + 1) * 128], vn[:, j, g * 128:(g + 1) * 128], wsT[:, l, g, :], True, True,
                       [b_vn, b_wsT], [bk_b], j == 3)
                ya, ya_b, ya_i = tfa()
                for j in range(4):
                    TT(ya[:, j * 128:(j + 1) * 128], bk[:, j * 128:(j + 1) * 128], lnv[:, 2, g * 128:(g + 1) * 128], ALU.add,
                       [bk_b, b_lnv], [ya_b])
                TT(ya[:, :], ya[:, :], uT[:, g, :], ALU.mult, [ya_b, b_uT[g]], [ya_b], eng=(pool if POOL_OFFLOAD else dve))
                spawn(tail_chain([ya[:, :]], [ya_b], [ya_i], [g], l, 2))
                yield 1

        spawn(chainA())
        w, w_b = w_next()
        for g in range(4):
            bk, bk_b = mmf(w, g * 128, hT, all_hT, w_b)
            CP(act, bgT[:, g, :], bk[:, :], [bk_b], [b_bg[g]])
        w_done()
        w, w_b = w_next()
        for g in range(4):
            bk, bk_b = mmf(w, g * 128, hT, all_hT, w_b)
            CP(act, cgT[:, g, :], bk[:, :], [bk_b], [b_cg[g]])
        w_done()
        w, w_b = w_next()
        for g in range(4):
            bk, bk_b = mmf(w, g * 128, hT, all_hT, w_b)
            xe, xe_b = xh[g % 2], b_xh[g % 2]
            CP(dve, xe[:, 0:2], carry[:, l, g, :], [b_carry[l][g]], [xe_b])
            TT(xe[:, 2:T + 2], bk[:, :], cgT[:, g, :], ALU.mult, [bk_b, b_cg[g]], [xe_b])
            CP(dve, carry[:, l, g, :], xe[:, T:T + 2], [xe_b], [b_carry[l][g]])
            y, y_b, y_i = tfa()
            cwb = l * 12 + g * 3
            ce = pool if POOL_OFFLOAD else dve
            TS(y[:, :], xe[:, 2:T + 2], cw[:, cwb + 2:cwb + 3], None, ALU.mult, None, [xe_b, b_cw], [y_b])
            STT(y[:, :], xe[:, 1:T + 1], cw[:, cwb + 1:cwb + 2], y[:, :], ALU.mult, ALU.add, [xe_b, b_cw, y_b], [y_b])
            STT(y[:, :], xe[:, 0:T], cw[:, cwb:cwb + 1], y[:, :], ALU.mult, ALU.add, [xe_b, b_cw, y_b], [y_b])
            TT(y[:, :], y[:, :], bgT[:, g, :], ALU.mult, [y_b, b_bg[g]], [y_b], eng=ce)
            spawn(tail_chain([y[:, :]], [y_b], [y_i], [12 + g], l, 3))
        w_done()
        cs_t, cs_b, sn_t, sn_b = cs_tile, b_cs, sn_tile, b_sn

        def rope_chain(bk, bk_b, dst, dst_b):
            raw, raw_b, raw_i = tba()
            CP(act, raw[:, :], bk[:, :], [bk_b], [raw_b])
            t1, t1_b, t1_i = tfa()
            TT(t1[:, :], bk[:, :], cs_t[:, :], ALU.mult, [bk_b, cs_b, raw_b], [t1_b])
            yield 2
            bk2, bk2_b = bank()
            MM(bk2[:, :], swap, raw[:, :], True, True, [raw_b, b_cbf], [bk2_b], True)
            tbf(raw_i)
            t2, t2_b, t2_i = tfa()
            TT(t2[:, :], bk2[:, :], sn_t[:, :], ALU.mult, [bk2_b, sn_b], [t2_b])
            TT(dst, t1[:, :], t2[:, :], ALU.add, [t1_b, t2_b], [dst_b], eng=(pool if POOL_OFFLOAD else dve))
            tff(t1_i)
            tff(t2_i)

        for blk in range(4):
            w, w_b = w_next()
            for cc in range(4):
                c = (blk % 2) * 4 + cc
                bk, bk_b = mmf(w, cc * 128, hT, all_hT, w_b)
                if blk < 2:
                    spawn(rope_chain(bk, bk_b, qT[:, c, :], b_qT[c]))
                else:
                    spawn(rope_chain(bk, bk_b, kTc[:, c, :], b_kT[c]))
            w_done()
        for blk in range(2):
            w, w_b = w_next()
            for j in range(4):
                bk, bk_b = mmt(w, 0, 512, hT, all_hT, j, w_b)
                CP(act, Vc[:, j, blk * 512:(blk + 1) * 512], bk[:, :], [bk_b], [b_V[j]])
            w_done()
        drain()
        state["bankset"] = [0, 1]
        state["bank"] = 0
        scale = 1.0 / math.sqrt(HD)
        accsets = [[(banks[2 + e], b_bank[2 + e]) for e in range(3)], [(banks[5 + e], b_bank[5 + e]) for e in range(3)]]
        pieces = [(h, jj, p) for h in range(4) for jj in range(2) for p in range(t + 1)]
        loaded = {}
        lstate = {"n": 0}

        def load_piece(idx):
            if idx >= len(pieces) or idx in loaded:
                return
            h, jj, p = pieces[idx]
            if p == t:
                loaded[idx] = None
                return
            i = lstate["n"] % NKP
            lstate["n"] += 1
            psl = slice(p * T, (p + 1) * T)
            k.dma(sp, kp[i][:, :], kc_d[l, 2 * h + jj, :, psl], reads=[b_kc[l]], writes=[b_kp[i]])
            k.dma(sp, vp[i][:, :, :], vc_d[l, psl, h * 256:(h + 1) * 256].rearrange("(j p) c -> p j c", p=128),
                  reads=[b_vc[l]], writes=[b_vp[i]])
            loaded[idx] = i

        blocks = [(pi, b) for pi in range(len(pieces)) for b in range(4)]
        Sinfo = {}
        esum = {}

        def emit_S(bi):
            pi, b = blocks[bi]
            h, jj, p = pieces[pi]
            ch = 2 * h + jj
            if b == 0:
                load_piece(pi)
                load_piece(pi + 1)
            if b == 2:
                load_piece(pi + 2)
            if p < t:
                i = loaded[pi]
                Kb, Kb_b, c0 = kp[i][:, b * 128:(b + 1) * 128], b_kp[i], 0
            else:
                Kb, Kb_b, c0 = kTc[:, ch, b * 128:(b + 1) * 128], b_kT[ch], b * 128
            sb, sb_b = bank()
            MM(sb[:, c0:T], Kb, qT[:, ch, c0:T], True, True, [Kb_b, b_qT[ch]], [sb_b], True)
            E, E_b, E_i = tba()
            ACT(E[:, c0:T], sb[:, c0:T], AF.Exp, [sb_b], [E_b], scale=scale)
            if p == t:
                TT(E[:, c0:T], E[:, c0:T], mask(b)[:, c0:T], ALU.mult, [E_b, b_cbf], [E_b])
            if ESUM:
                if p == 0 and b == 0:
                    Es, Es_b, Es_i = tfa()
                    esum[(h, jj)] = (Es, Es_b, Es_i)
                    CP(pool, Es[:, :], E[:, :], [E_b], [Es_b])
                else:
                    Es, Es_b, Es_i = esum[(h, jj)]
                    TT(Es[:, c0:T], Es[:, c0:T], E[:, c0:T], ALU.add, [Es_b, E_b], [Es_b], eng=pool)
            Sinfo[bi] = (E, E_b, E_i, c0)

        o1 = [o1a, o1b]

        def emit_AV(bi):
            pi, b = blocks[bi]
            h, jj, p = pieces[pi]
            E, E_b, E_i, c0 = Sinfo.pop(bi)
            acc = accsets[jj]
            if p < t:
                i = loaded[pi]
                Vb, Vb_b = vp[i][:, b, :], b_vp[i]
            else:
                Vb, Vb_b = Vc[:, b, h * 256:(h + 1) * 256], b_V[b]
            first = (p == 0 and b == 0)
            last = (p == t and b == 3)
            MM(acc[0][0][:, c0:T], Vb[:, 0:128], E[:, c0:T], first, last, [Vb_b, E_b], [acc[0][1]] if (first or last) else [], False)
            MM(acc[1][0][:, c0:T], Vb[:, 128:256], E[:, c0:T], first, last, [Vb_b, E_b], [acc[1][1]] if (first or last) else [], ESUM)
            if not ESUM:
                MM(acc[2][0][:, c0:T], ones, E[:, c0:T], first, last, [E_b, b_cbf], [acc[2][1]] if (first or last) else [], True)
            tbf(E_i)
            if ESUM and last:
                Es, Es_b, Es_i = esum.pop((h, jj))
                Esb, Esb_b, Esb_i = tba()
                CP(act, Esb[:, :], Es[:, :], [Es_b], [Esb_b])
                MM(acc[2][0][:, :], ones, Esb[:, :], True, True, [Esb_b, b_cbf], [acc[2][1]], True)
                tbf(Esb_i)
                tff(Es_i)
            if last:
                r, r_b, r_i = tfa()
                if ACT_RSQRT:
                    ACT(r[:, :], acc[2][0][:, :], AF.Ln, [acc[2][1]], [r_b])
                    ACT(r[:, :], r[:, :], AF.Exp, [r_b], [r_b], scale=-1.0)
                else:
                    RECIPF(r[:, :], acc[2][0][:, :], [acc[2][1]], [r_b])
                if jj == 0:
                    for e in range(2):
                        TT(o1[e][:, :], acc[e][0][:, :], r[:, :], ALU.mult, [acc[e][1], r_b], [b_o1[e]])
                    tff(r_i)
                else:
                    for e in range(2):
                        bb, bb_b, bb_i = tfa()
                        TT(bb[:, :], acc[e][0][:, :], r[:, :], ALU.mult, [acc[e][1], r_b], [bb_b])
                        STT(o1[e][:, :], bb[:, :], nlam[:, l:l + 1], o1[e][:, :], ALU.mult, ALU.add, [bb_b, b_o1[e], b_nlam], [b_o1[e]])
                        tff(bb_i)
                    tff(r_i)
                    spawn(tail_chain([o1[0][:, :], o1[1][:, :]], [b_o1[0], b_o1[1]], [None, None], [4 + 2 * h, 5 + 2 * h], l, (2 if t == 0 else 3), pre=(1 if t == 0 else 4), sq_on_dve=True))

        nblk = len(blocks)
        emit_S(0)
        for bi in range(nblk):
            if bi + 1 < nblk:
                emit_S(bi + 1)
            emit_AV(bi)
            tick()
        state["bankset"] = list(range(8))
        load_gbc(2 * l + 1)
        if t < NT - 1:
            k.dma(sp, kc_d[l, :, :, sl].rearrange("c d t -> d c t"), kTc[:, :, :], reads=[b_kT], writes=[b_kc[l]])
            k.dma(sp, vc_d[l, sl, :].rearrange("(j p) c -> p j c", p=128), Vc[:, :, :], reads=[b_V], writes=[b_vc[l]])
        oorder = [0, 1, 2, 3, 12, 13, 14, 15, 4, 5, 6, 7, 8, 9, 10, 11]
        for nb in range(4):
            w, w_b = w_next()
            wv = w[:, :].rearrange("p (k c) -> p k c", k=16)

            def omm(bk, bk_b, j, i):
                kc = oorder[i]
                MM(bk[:, :], mixT[:, kc, j * 128:(j + 1) * 128], wv[:, kc, :], i == 0, i == 15,
                   [w_b, b_mix[kc]], [bk_b] if i in (0, 15) else [], i == 15)

            def oadd(bk, bk_b, j):
                TT(xs[:, j, nb * 512:(nb + 1) * 512], xs[:, j, nb * 512:(nb + 1) * 512], bk[:, :], ALU.add,
                   [b_x[j], bk_b], [b_x[j]])
                if nb == 3:
                    norm_pre(j, True)
                    if j >= 2:
                        norm_stt(j - 2)

            if nb == 0:
                grp = []
                for j in range(4):
                    bk, bk_b = bank()
                    grp.append((bk, bk_b))
                    for i in range(14):
                        omm(bk, bk_b, j, i)
                    tick()
                drain()
                for j in range(4):
                    bk, bk_b = grp[j]
                    omm(bk, bk_b, j, 14)
                    omm(bk, bk_b, j, 15)
                    oadd(bk, bk_b, j)
            else:
                for j in range(4):
                    bk, bk_b = bank()
                    for i in range(16):
                        omm(bk, bk_b, j, i)
                    oadd(bk, bk_b, j)
            w_done()
        norm_stage(2 * l + 1)
        if l + 1 < L:
            load_gbc(2 * (l + 1))
        else:
            load_gbc_fin()
            if t + 1 < NT:
                load_gbc(0)

        for fg in range(11):
            wg_, wg_b = w_next()
            wu_, wu_b = w_next()
            aT, aT_b = actT[fg % 2], b_actTc[fg % 2]
            for c in range(4):
                bg_, bg_b = mm_feat(wg_, c * 128, hT, all_hT, wg_b)
                bu_, bu_b = mm_feat(wu_, c * 128, hT, all_hT, wu_b)
                sg, sg_b, sg_i = tfa()
                ACT(sg[:, :], bg_[:, :], AF.Silu, [bg_b], [sg_b])
                TT(aT[:, c, :], sg[:, :], bu_[:, :], ALU.mult, [sg_b, bu_b], [aT_b[c], b_actT[fg % 2]])
                tff(sg_i)
            w_done(2)
            wd_, wd_b = w_next()
            wdv = wd_[:, :].rearrange("p (c n) -> p c n", c=4)
            groups = [(j, nb) for j in range(4) for nb in range(4)]

            def dmm(bk, bk_b, j, nb, c):
                MM(bk[:, :], aT[:, c, j * 128:(j + 1) * 128], wdv[:, c, nb * 512:(nb + 1) * 512], c == 0, c == 3,
                   [aT_b[c], wd_b], [bk_b] if c in (0, 3) else [], c == 3)

            def dadd(bk, bk_b, j, nb):
                TT(xs[:, j, nb * 512:(nb + 1) * 512], xs[:, j, nb * 512:(nb + 1) * 512], bk[:, :], ALU.add,
                   [b_x[j], bk_b], [b_x[j]])
                if fg == 10 and nb == 3:
                    norm_pre(j, l + 1 < L)
                    if j >= 2 and l + 1 < L:
                        norm_stt(j - 2)

            head = []
            for (j, nb) in groups[:4]:
                bk, bk_b = bank()
                head.append((bk, bk_b, j, nb))
                for c in range(3):
                    dmm(bk, bk_b, j, nb, c)
            for (bk, bk_b, j, nb) in head:
                dmm(bk, bk_b, j, nb, 3)
                dadd(bk, bk_b, j, nb)
            for (j, nb) in groups[4:]:
                bk, bk_b = bank()
                for c in range(4):
                    dmm(bk, bk_b, j, nb, c)
                dadd(bk, bk_b, j, nb)
            w_done()
        drain()

    hT_f = hT[:, :, :].rearrange("p c t -> p (c t)").bitcast(F32)
    mixT_f = mixT[:, :, :].rearrange("p c t -> p (c t)").bitcast(F32)

    def load_x(t, j):
        r0 = t * T + j * 128
        k.dma(sp, xs[:, j, :], x_d[r0:r0 + 128, :], writes=[b_x[j]])

    load_gbc(0)
    for j in range(4):
        load_x(0, j)
        norm_pre(j, True)
        if j < 2:
            norm_stt(j)
    for t in range(NT):
        sl = slice(t * T, (t + 1) * T)
        for l in range(L):
            layer_tile(t, l)
        for j in range(4):
            stg, stg_b = (hT_f, all_hT) if j < 2 else (mixT_f, b_mix)
            STTe(dve, stg[:, (j % 2) * D:(j % 2 + 1) * D], xs[:, j, :], st[:, 4 + j:5 + j], gbc_fin,
                 ALU.mult, ALU.mult, [b_x[j], b_rstd[j], b_gbcf], [stg_b])
            if t + 1 < NT:
                load_x(t + 1, j)
        for half, (stg, stg_b) in enumerate(((hT_f, all_hT), (mixT_f, b_mix))):
            r0 = t * T + half * 256
            k.dma(sp, out_d[r0:r0 + 256, :].rearrange("(j p) d -> p j d", p=128),
                  stg[:, :].rearrange("p (j d) -> p j d", j=2), reads=[stg_b], writes=[b_out])
        if t + 1 < NT:
            for j in range(4):
                norm_pre(j, True)
                if j < 2:
                    norm_stt(j)
    k.finish([b_out])
    return nc


def prep_shared(inp, L):
    f = lambda a: np.ascontiguousarray(np.asarray(a, dtype=np.float32))
    cbf, cf = make_consts()
    gv = [None] * (2 * L + 1)
    for l in range(L):
        gv[2 * l] = np.asarray(inp["attn_norm"])[l]
        gv[2 * l + 1] = np.asarray(inp["ffn_norm"])[l]
    gv[2 * L] = np.asarray(inp["final_norm"])
    gvec = f(np.stack(gv))
    lnv = []
    for l in range(L):
        lnv += [np.asarray(inp["gmlp_ln_g"])[l].reshape(512), np.asarray(inp["gmlp_ln_b"])[l].reshape(512),
                np.asarray(inp["gmlp_bs"])[l].reshape(512)]
    lnv = f(np.stack(lnv))
    ws = f(np.asarray(inp["gmlp_ws"])[:L])
    lam = f(np.concatenate([np.asarray(inp[n])[:L] for n in ("lambda_q1", "lambda_k1", "lambda_q2", "lambda_k2")], axis=1))
    cwv = np.asarray(inp["conv_w"])[:L]
    cw = f(cwv.reshape(L, 3, 4, 128).transpose(3, 0, 2, 1).reshape(128, L * 12))
    mnv = np.asarray(inp["mix_norm"])[:L]
    mn = f(mnv.reshape(L, 16, 128).transpose(2, 0, 1).reshape(128, L * 16))

    def colblocks(w, nb):
        w = np.asarray(w)[:L]
        return f(w.reshape(L, 16, 128, nb, 512).transpose(0, 3, 2, 1, 4).reshape(L, nb, 128, 8192))

    win = colblocks(inp["w_in"], 11)
    wout = colblocks(inp["w_out"], 4)
    wg = colblocks(inp["w_gate"], 11)
    wu = colblocks(inp["w_up"], 11)
    wdn = np.asarray(inp["w_down"])[:L]
    wd = f(wdn.reshape(L, 11, 4, 128, 2048).transpose(0, 1, 3, 2, 4).reshape(L, 11, 128, 8192))
    return dict(cbf=cbf, cf=cf, gvec=gvec, lnv=lnv, ws=ws, lam=lam, cw=cw, mn=mn, win=win, wout=wout, wg=wg, wu=wu, wd=wd)


def run(inp, S, L, ncores, trace=False, stop=99):
    shared = prep_shared(inp, L)
    x = np.asarray(inp["x"], dtype=np.float32)
    pos = np.asarray(inp["positions"]).astype(np.int32)
    nc = build(S, L, stop)
    in_maps = []
    for b in range(ncores):
        m = dict(shared)
        m["x"] = np.ascontiguousarray(x[b, :S])
        m["pos"] = np.ascontiguousarray(pos[b:b + 1, :S])
        in_maps.append(m)
    res = run_bass_kernel_spmd(nc, in_maps, core_ids=list(range(ncores)), trace=trace)
    out = np.stack([np.asarray(r["out"], dtype=np.float32) for r in res.results])
    return out, res


def kernel(**inputs):
    out, _ = run(inputs, 4096, 2, 8)
    return out
```
